# Optimizing a Trainium2 kernel written in Bass

```python
import math
import jax, jax.numpy as jnp
from jax import lax
import numpy as np

D_MODEL = 2048
BATCH = 4
SEQ = 4096
DEPTH = 4

CHUNK = 64
N_MIXERS = 3
EPS = 1e-6
DA_HEAD_DIM = 128
DA_HEADS = D_MODEL // (2 * DA_HEAD_DIM)
DA_VALUE_DIM = 2 * DA_HEAD_DIM
Q_BLOCK = 128
S5_GROUP = 16
S5_GROUPS = D_MODEL // S5_GROUP
S5_STATE = 64
S5_DT_MIN = 0.001
S5_DT_MAX = 0.1
RET_QK_DIM = 256
RET_HEADS = D_MODEL // RET_QK_DIM
RET_V_DIM = 2 * RET_QK_DIM
ROPE_BASE = 10000.0
D_FF = 4 * D_MODEL
N_A = (DEPTH + N_MIXERS - 1) // N_MIXERS
N_B = (DEPTH + N_MIXERS - 2) // N_MIXERS
N_C = DEPTH // N_MIXERS

kernel_name = "hybrid_diffattn_s5_retention_sqrelu"

F32 = jnp.float32


def rmsnorm(x, g):
    xf = x.astype(F32)
    y = xf * lax.rsqrt(jnp.mean(xf * xf, axis=-1, keepdims=True) + EPS)
    return (y * g.astype(F32)).astype(x.dtype)


def rotary(t, pos):
    half = t.shape[-1] // 2
    inv = 1.0 / (ROPE_BASE ** jnp.linspace(0.0, 1.0, half, dtype=F32))
    ang = pos[:, None] * inv[None, :]
    cos = jnp.cos(ang)[None, :, None, :]
    sin = jnp.sin(ang)[None, :, None, :]
    t1, t2 = t[..., :half], t[..., half:]
    return jnp.concatenate([t1 * cos - t2 * sin, t1 * sin + t2 * cos], axis=-1)


def diff_attention(h, w_in, lam_p, subln_g, w_out, lambda_init):
    B, S, D = h.shape
    q, k, v = jnp.split(h @ w_in, 3, axis=-1)
    q = q.reshape(B, S, DA_HEADS, 2, DA_HEAD_DIM).astype(F32) * DA_HEAD_DIM ** -0.5
    k = k.reshape(B, S, DA_HEADS, 2, DA_HEAD_DIM).astype(F32)
    v = v.reshape(B, S, DA_HEADS, DA_VALUE_DIM).astype(F32)
    lp = lam_p.astype(F32)
    lam = jnp.exp(jnp.sum(lp[0] * lp[1])) - jnp.exp(jnp.sum(lp[2] * lp[3])) + lambda_init
    n_blk = S // Q_BLOCK
    k_chunk = jnp.arange(S) // CHUNK
    q_blocks = q.reshape(B, n_blk, Q_BLOCK, DA_HEADS, 2, DA_HEAD_DIM).swapaxes(0, 1)

    def block(args):
        qb, bi = args
        q_chunk = (bi * Q_BLOCK + jnp.arange(Q_BLOCK)) // CHUNK
        mask = k_chunk[None, :] <= q_chunk[:, None]
        s = jnp.einsum('bqhtd,bkhtd->bhtqk', qb, k)
        p = jax.nn.softmax(jnp.where(mask, s, -jnp.inf), axis=-1)
        pd = p[:, :, 0] - lam * p[:, :, 1]
        return jnp.einsum('bhqk,bkhe->bqhe', pd, v)

    o = lax.map(block, (q_blocks, jnp.arange(n_blk)))
    o = o.swapaxes(0, 1).reshape(B, S, DA_HEADS, DA_VALUE_DIM)
    o = o * lax.rsqrt(jnp.mean(o * o, axis=-1, keepdims=True) + EPS)
    o = o * subln_g.astype(F32) * (1.0 - lambda_init)
    return o.reshape(B, S, D).astype(h.dtype) @ w_out


def _scan_combine(first, second):
    a1r, a1i, b1r, b1i = first
    a2r, a2i, b2r, b2i = second
    ar = a2r * a1r - a2i * a1i
    ai = a2r * a1i + a2i * a1r
    br = a2r * b1r - a2i * b1i + b2r
    bi = a2r * b1i + a2i * b1r + b2i
    return (ar, ai, br, bi)


def s5_mixer(h, a_re, a_im, log_dt, b_re, b_im, c_re, c_im, d_skip, w_glu):
    B, S, D = h.shape
    lam_re = a_re.astype(F32)
    lam_im = a_im.astype(F32)
    dt = jnp.exp(log_dt.astype(F32))[:, None]
    mag = jnp.exp(lam_re * dt)
    ab_re = mag * jnp.cos(lam_im * dt)
    ab_im = mag * jnp.sin(lam_im * dt)
    den = lam_re * lam_re + lam_im * lam_im
    nr, ni = ab_re - 1.0, ab_im
    coef_re = (nr * lam_re + ni * lam_im) / den
    coef_im = (ni * lam_re - nr * lam_im) / den
    br, bim = b_re.astype(F32), b_im.astype(F32)
    bb_re = coef_re[..., None] * br - coef_im[..., None] * bim
    bb_im = coef_re[..., None] * bim + coef_im[..., None] * br
    cr, ci = c_re.astype(F32), c_im.astype(F32)
    a_seq_re = jnp.broadcast_to(ab_re, (S, S5_GROUPS, S5_STATE))
    a_seq_im = jnp.broadcast_to(ab_im, (S, S5_GROUPS, S5_STATE))
    u = h.astype(F32).reshape(B, S, S5_GROUPS, S5_GROUP)

    def per_sequence(u_s):
        bu_re = jnp.einsum('sgc,gpc->sgp', u_s, bb_re)
        bu_im = jnp.einsum('sgc,gpc->sgp', u_s, bb_im)
        _, _, x_re, x_im = lax.associative_scan(
            _scan_combine, (a_seq_re, a_seq_im, bu_re, bu_im), axis=0)
        return jnp.einsum('sgp,gcp->sgc', x_re, cr) - jnp.einsum('sgp,gcp->sgc', x_im, ci)

    y = lax.map(per_sequence, u).reshape(B, S, D)
    y = y + d_skip.astype(F32) * h.astype(F32)
    g = jax.nn.gelu(y).astype(h.dtype)
    val, gate = jnp.split(g @ w_glu, 2, axis=-1)
    return val * jax.nn.sigmoid(gate)


def retention_mixer(h, w_in, w_out):
    B, S, D = h.shape
    proj = h @ w_in
    q, k, v, g = jnp.split(proj, [D, 2 * D, 4 * D], axis=-1)
    pos = jnp.arange(S, dtype=F32)
    q = rotary(q.reshape(B, S, RET_HEADS, RET_QK_DIM).astype(F32), pos)
    k = rotary(k.reshape(B, S, RET_HEADS, RET_QK_DIM).astype(F32), pos) * RET_QK_DIM ** -0.5
    v = v.reshape(B, S, RET_HEADS, RET_V_DIM).astype(F32)
    log_gamma = jnp.log(1.0 - jnp.exp2(-5.0 - jnp.arange(RET_HEADS, dtype=F32)))
    idx = jnp.arange(CHUNK, dtype=F32)
    intra_decay = jnp.exp(log_gamma[:, None, None] * jnp.abs(idx[:, None] - idx[None, :]))
    q_decay = jnp.exp(log_gamma[None, :] * (idx[:, None] + 1.0))
    k_decay = jnp.exp(log_gamma[None, :] * (CHUNK - 1.0 - idx[:, None]))
    chunk_decay = jnp.exp(log_gamma * CHUNK)
    n_ch = S // CHUNK

    def to_chunks(t):
        return t.reshape(B, n_ch, CHUNK, *t.shape[2:]).swapaxes(0, 1)

    def step(R, inp):
        qc, kc, vc = inp
        s = jnp.einsum('bnhd,bmhd->bhnm', qc, kc) * intra_decay
        o = jnp.einsum('bhnm,bmhe->bnhe', s, vc)
        o = o + jnp.einsum('bnhd,bhde->bnhe', qc * q_decay[..., None], R)
        R = R * chunk_decay[None, :, None, None] + jnp.einsum(
            'bmhd,bmhe->bhde', kc * k_decay[..., None], vc)
        return R, o

    R0 = jnp.zeros((B, RET_HEADS, RET_QK_DIM, RET_V_DIM), F32)
    _, o = lax.scan(step, R0, (to_chunks(q), to_chunks(k), to_chunks(v)))
    o = o.swapaxes(0, 1).reshape(B, S, RET_HEADS, RET_V_DIM)
    o = o * lax.rsqrt(jnp.mean(o * o, axis=-1, keepdims=True) + EPS)
    o = (jax.nn.silu(g.astype(F32)) * o.reshape(B, S, RET_HEADS * RET_V_DIM)).astype(h.dtype)
    return o @ w_out


def sqrelu_mlp(h, w1, w2):
    a = jax.nn.relu(h @ w1)
    return (a * a) @ w2


def setup_inputs(seed: int = 0) -> dict:
    key = jax.random.key(seed)
    ks = jax.random.split(key, 24)
    D = D_MODEL

    def w(k, shape, fan_in):
        return jax.random.normal(k, shape, F32) * fan_in ** -0.5

    def gain(k, shape):
        return 1.0 + 0.02 * jax.random.normal(k, shape, F32)

    G, P, C = S5_GROUPS, S5_STATE, S5_GROUP
    return {
        "x": jax.random.normal(ks[0], (BATCH, SEQ, D), F32),
        "norm_mix": gain(ks[1], (DEPTH, D)),
        "norm_mlp": gain(ks[2], (DEPTH, D)),
        "norm_final": gain(ks[3], (D,)),
        "a_w_in": w(ks[4], (N_A, D, 3 * D), D),
        "a_lambda": 0.1 * jax.random.normal(ks[5], (N_A, 4, DA_HEAD_DIM), F32),
        "a_subln": gain(ks[6], (N_A, DA_VALUE_DIM)),
        "a_w_out": w(ks[7], (N_A, D, D), D),
        "b_a_re": -0.5 + 0.01 * jax.random.normal(ks[8], (N_B, G, P), F32),
        "b_a_im": jnp.pi * jnp.arange(P, dtype=F32)[None, None, :]
                  + 0.01 * jax.random.normal(ks[9], (N_B, G, P), F32),
        "b_log_dt": jax.random.uniform(ks[10], (N_B, G), F32,
                                       math.log(S5_DT_MIN), math.log(S5_DT_MAX)),
        "b_b_re": w(ks[11], (N_B, G, P, C), 2 * C),
        "b_b_im": w(ks[12], (N_B, G, P, C), 2 * C),
        "b_c_re": w(ks[13], (N_B, G, C, P), 2 * P),
        "b_c_im": w(ks[14], (N_B, G, C, P), 2 * P),
        "b_d": jax.random.normal(ks[15], (N_B, D), F32),
        "b_w_glu": w(ks[16], (N_B, D, 2 * D), D),
        "c_w_in": w(ks[17], (N_C, D, 6 * D), D),
        "c_w_out": w(ks[18], (N_C, 2 * D, D), 2 * D),
        "mlp_w1": w(ks[19], (DEPTH, D, D_FF), D),
        "mlp_w2": w(ks[20], (DEPTH, D_FF, D), D_FF),
    }


def reference(x, norm_mix, norm_mlp, norm_final, a_w_in, a_lambda, a_subln, a_w_out,
              b_a_re, b_a_im, b_log_dt, b_b_re, b_b_im, b_c_re, b_c_im, b_d, b_w_glu,
              c_w_in, c_w_out, mlp_w1, mlp_w2):
    for i in range(DEPTH):
        kind = i % N_MIXERS
        j = i // N_MIXERS
        h = rmsnorm(x, norm_mix[i])
        if kind == 0:
            lambda_init = 0.8 - 0.6 * math.exp(-0.3 * i)
            mix = diff_attention(h, a_w_in[j], a_lambda[j], a_subln[j], a_w_out[j], lambda_init)
        elif kind == 1:
            mix = s5_mixer(h, b_a_re[j], b_a_im[j], b_log_dt[j], b_b_re[j], b_b_im[j],
                           b_c_re[j], b_c_im[j], b_d[j], b_w_glu[j])
        else:
            mix = retention_mixer(h, c_w_in[j], c_w_out[j])
        x = x + mix
        x = x + sqrelu_mlp(rmsnorm(x, norm_mlp[i]), mlp_w1[i], mlp_w2[i])
    return rmsnorm(x, norm_final)
```

```python
import contextlib
import os as _os
import math
import numpy as np
import concourse.bass as bass
import concourse.mybir as mybir
from concourse.alu_op_type import AluOpType as ALU
from concourse.bass_utils import run_bass_kernel_spmd

AF = mybir.ActivationFunctionType
AX = mybir.AxisListType
F32 = mybir.dt.float32
BF16 = mybir.dt.bfloat16
I32 = mybir.dt.int32
NDS = 8

D = 2048
NDC = 16
EPS = 1e-6
TWO_PI = 2.0 * math.pi


class Tok:
    __slots__ = ("name", "w", "r")

    def __init__(self, name=""):
        self.name = name
        self.w = None
        self.r = []


class Buf:
    __slots__ = ("t", "k")

    def __init__(self, t):
        self.t = t
        self.k = Tok()


class Prog:
    def __init__(self, nc, stack):
        self.nc = nc
        self.st = stack
        self.engs = {"pe": nc.tensor, "dve": nc.vector, "act": nc.scalar,
                     "pool": nc.gpsimd, "sp": nc.sync}
        self.sem = {k: stack.enter_context(nc.semaphore("c_" + k)) for k in self.engs}
        self.cnt = {k: 0 for k in self.engs}
        self.seen = {k: {} for k in self.engs}
        self.dsem = {q: [stack.enter_context(nc.semaphore("d_%s%d" % (q, i))) for i in range(NDS)]
                     for q in ("sp", "pool", "act")}
        self.dcnt = {q: 0 for q in self.dsem}
        self.ntile = 0
        self.nins = 0

    def sb(self, shape, dtype, st=None):
        self.ntile += 1
        t = (st or self.st).enter_context(self.nc.sbuf_tensor("sb%d" % self.ntile, list(shape), dtype))
        return Buf(t)

    def ps(self, shape, dtype=F32):
        self.ntile += 1
        t = self.st.enter_context(self.nc.psum_tensor("ps%d" % self.ntile, list(shape), dtype))
        return Buf(t)

    def _wait(self, E, ev):
        sem, val = ev
        key = id(sem)
        if self.seen[E].get(key, 0) >= val:
            return
        self.engs[E].wait_ge(sem, val)
        self.seen[E][key] = val

    def _deps(self, E, reads, writes):
        best = {}

        def add(ev):
            k = id(ev[0])
            if k not in best or best[k][1] < ev[1]:
                best[k] = ev
        for t in reads:
            if t.w is not None:
                add(t.w)
        for t in writes:
            if t.w is not None:
                add(t.w)
            for ev in t.r:
                add(ev)
        for ev in best.values():
            if E == "pe" and ev[0] is self.sem["pe"]:
                continue
            self._wait(E, ev)

    def _commit(self, ev, reads, writes):
        for t in reads:
            t.r.append(ev)
            if len(t.r) > 48:
                best = {}
                for sem, val in t.r:
                    k = id(sem)
                    if k not in best or best[k][1] < val:
                        best[k] = (sem, val)
                t.r = list(best.values())
        for t in writes:
            t.w = ev
            t.r = []

    def op(self, E, fn, reads=(), writes=(), inc=True):
        reads = [b.k if isinstance(b, Buf) else b for b in reads]
        writes = [b.k if isinstance(b, Buf) else b for b in writes]
        self._deps(E, reads, writes)
        ins = fn(self.engs[E])
        self.nins += 1
        if inc:
            self.cnt[E] += 1
            ins.then_inc(self.sem[E], 1)
            ev = (self.sem[E], self.cnt[E])
        else:
            ev = (self.sem[E], self.cnt[E] + 1)
        self._commit(ev, reads, writes)
        return ev

    def dma(self, q, out, in_, reads=(), writes=(), **kw):
        reads = [b.k if isinstance(b, Buf) else b for b in reads]
        writes = [b.k if isinstance(b, Buf) else b for b in writes]
        i = self.dcnt[q]
        s = self.dsem[q][i % NDS]
        if i >= NDS:
            self._wait(q, (s, 16 * (i // NDS)))
        self._deps(q, reads, writes)
        self.engs[q].dma_start(out=out, in_=in_, **kw).then_inc(s, 16)
        self.nins += 1
        ev = (s, 16 * (i // NDS + 1))
        self.dcnt[q] += 1
        self._commit(ev, reads, writes)
        return ev

    def cc(self, kind, op, groups, in_ap, out_ap, reads=(), writes=()):
        reads = [b.k if isinstance(b, Buf) else b for b in reads]
        writes = [b.k if isinstance(b, Buf) else b for b in writes]
        q = "pool"
        i = self.dcnt[q]
        s = self.dsem[q][i % NDS]
        if i >= NDS:
            self._wait(q, (s, 16 * (i // NDS)))
        self._deps(q, reads, writes)
        self.nc.gpsimd.collective_compute(kind, op, replica_groups=groups, ins=[in_ap], outs=[out_ap]).then_inc(s, 16)
        self.nins += 1
        ev = (s, 16 * (i // NDS + 1))
        self.dcnt[q] += 1
        self._commit(ev, reads, writes)
        return ev

    def barrier(self):
        evs = []
        for k in self.engs:
            if self.cnt[k] > 0:
                evs.append((self.sem[k], self.cnt[k]))
        for q, l in self.dsem.items():
            n = self.dcnt[q]
            for j, s in enumerate(l):
                c = (n - j + NDS - 1) // NDS
                if c > 0:
                    evs.append((s, 16 * c))
        for E in self.engs:
            for ev in evs:
                if ev[0] is self.sem[E]:
                    continue
                self._wait(E, ev)


class Ctx:
    pass


def dram_in(nc, name, shape, dtype=F32):
    return nc.dram_tensor(name, list(shape), dtype, kind="ExternalInput").ap()


def build_program(S, layers, FF, dbg_x=False, dbg_mode=None):
    nc = bass.Bass("TRN2", target_bir_lowering=False)
    NT = S // 128
    NB = S // 512
    c = Ctx()
    c.nc, c.S, c.NT, c.NB, c.FF = nc, S, NT, NB, FF
    c.x_in = dram_in(nc, "x", [S, D])
    c.gains = dram_in(nc, "gains", [2 * len(layers) + 1, D])
    c.ident_in = dram_in(nc, "ident", [128, 128])
    c.y = nc.dram_tensor("y", [S, D], F32, kind="ExternalOutput").ap()
    c.X = nc.dram_tensor("Xs", [S, D], F32, kind="Internal").ap()
    c.MIX = nc.dram_tensor("MIXs", [S, D], F32, kind="Internal").ap()
    c.HT = nc.dram_tensor("HTs", [NDC, 128, S], BF16, kind="Internal").ap()
    c.Xk = [[Tok() for _ in range(4)] for _ in range(NT)]
    c.MIXk = [Tok() for _ in range(NT)]
    c.HTk = [Tok() for _ in range(NT)]
    c.yk = [Tok() for _ in range(NT)]
    c.OT = nc.dram_tensor("OTs", [32, 128, S], BF16, kind="Internal").ap()
    c.OTk = [Tok() for _ in range(NT)]
    NFG_ = FF // 512
    c.W1S = nc.dram_tensor("W1Ss", [NFG_, 128, NDC * 512], BF16, kind="Internal").ap()
    c.W2S = nc.dram_tensor("W2Ss", [4 * NFG_, 128, 4 * 512], BF16, kind="Internal").ap()
    c.W1Sk = [Tok() for _ in range(NFG_)]
    c.W2Sk = [Tok() for _ in range(4 * NFG_)]
    for li, L in enumerate(layers):
        L["w1"] = dram_in(nc, "w1_%d" % li, [D, FF])
        L["w2"] = dram_in(nc, "w2_%d" % li, [FF, D])
        if L["kind"] == "a":
            NH = L["NH"]
            L["wq"] = dram_in(nc, "a_wq_%d" % li, [D, NH * 256])
            L["wk"] = dram_in(nc, "a_wk_%d" % li, [D, NH * 256])
            L["wv"] = dram_in(nc, "a_wv_%d" % li, [D, NH * 256])
            L["wo"] = dram_in(nc, "a_wo_%d" % li, [NH * 256, D])
            L["lam"] = dram_in(nc, "a_lam_%d" % li, [4, 128])
            L["subln"] = dram_in(nc, "a_subln_%d" % li, [256])
        if L["kind"] == "b":
            NG = L["NG"]
            L["a_re"] = dram_in(nc, "b_a_re", [NG, 64])
            L["a_im"] = dram_in(nc, "b_a_im", [NG, 64])
            L["log_dt"] = dram_in(nc, "b_log_dt", [NG])
            L["b_re"] = dram_in(nc, "b_b_re", [NG, 64, 16])
            L["b_im"] = dram_in(nc, "b_b_im", [NG, 64, 16])
            L["c_re"] = dram_in(nc, "b_c_re", [NG, 16, 64])
            L["c_im"] = dram_in(nc, "b_c_im", [NG, 16, 64])
            L["d"] = dram_in(nc, "b_d", [NG * 16])
            L["wglu"] = dram_in(nc, "b_wglu", [NG * 16, 2 * D])
            L["iota"] = dram_in(nc, "b_iota", [1, 512])
            c.BBS = nc.dram_tensor("BBSs", [2, NG, 16, 64], BF16, kind="Internal").ap()
        if L["kind"] == "c":
            NH = L["NH"]
            L["wq"] = dram_in(nc, "c_wq_%d" % li, [D, NH * 256])
            L["wk"] = dram_in(nc, "c_wk_%d" % li, [D, NH * 256])
            L["wv"] = dram_in(nc, "c_wv_%d" % li, [D, NH * 512])
            L["wg"] = dram_in(nc, "c_wg_%d" % li, [D, NH * 512])
            L["wo"] = dram_in(nc, "c_wo_%d" % li, [NH * 512, D])
            L["cos"] = dram_in(nc, "c_cos", [128, S])
            L["sin"] = dram_in(nc, "c_sin", [128, S])
            L["dtab"] = dram_in(nc, "c_dtab", [8, 4, 128, 512])
            L["qdec"] = dram_in(nc, "c_qdec", [8, 512])
            L["kdecT"] = dram_in(nc, "c_kdecT", [8, 128, 4])
    with contextlib.ExitStack() as st:
        p = Prog(nc, st)
        c.p = p
        c.psb = [p.ps([128, 512], F32) for _ in range(8)]
        c.ident = p.sb([128, 128], BF16)
        with contextlib.ExitStack() as ph:
            idf = p.sb([128, 128], F32, ph)
            p.dma("sp", idf.t[:], c.ident_in, writes=[idf])
            p.op("dve", lambda e: e.tensor_copy(out=c.ident.t[:], in_=idf.t[:]), reads=[idf], writes=[c.ident])
            p.barrier()
        first = True
        if dbg_mode == "mlp_nomix":
            phase_norm(c, 0, add_mix=False, from_input=True)
            phase_mlp(c, layers[0])
            phase_norm(c, 1, add_mix=False, from_input=False, final=True)
            layers = []
        if dbg_mode == "norm_only":
            phase_norm(c, 0, add_mix=False, from_input=True)
            phase_norm(c, 1, add_mix=False, from_input=False, final=True)
            layers = []
        for li, L in enumerate(layers):
            L["conv"] = make_conv(c, L)
            if L["kind"] != "none":
                phase_norm(c, 2 * li, add_mix=False, from_input=first)
                first = False
                if L["kind"] == "a":
                    phase_attn(c, L)
                    phase_outproj(c, L["wo"], L["NH"] * 2)
                elif L["kind"] == "b":
                    phase_s5(c, L)
                    phase_glu(c, L["wglu"], L["NG"] // 8)
                elif L["kind"] == "c":
                    phase_ret(c, L)
                    phase_outproj(c, L["wo"], L["NH"] * 4)
                else:
                    raise NotImplementedError
            phase_norm(c, 2 * li + 1, add_mix=False, from_input=first)
            first = False
            phase_mlp(c, L)
        if dbg_mode is None:
            phase_norm(c, 2 * len(layers), add_mix=False, from_input=False, final=True)
        for t in c.yk:
            if t.w is not None:
                p._wait("sp", t.w)
        p.barrier()
    c.nins = p.nins
    return nc, c


def make_conv(c, L):
    NFG = c.FF // 512
    w1v = L["w1"].rearrange("(k p) f -> p k f", p=128)
    w2v = L["w2"].rearrange("(f p) d -> p f d", p=128)
    jobs = [("w1", 0, fg) for fg in range(NFG)] + [("w2", cg, fg) for cg in range(4) for fg in range(NFG)]
    state = {"i": 0}

    def conv(part, nparts):
        end = len(jobs) * (part + 1) // nparts
        while state["i"] < end:
            kind, cg, fg = jobs[state["i"]]
            state["i"] += 1
            if kind == "w1":
                c.p.dma("pool", c.W1S[fg].rearrange("p (k f) -> p k f", k=NDC), w1v[:, :, fg * 512:(fg + 1) * 512],
                        writes=[c.W1Sk[fg]])
            else:
                c.p.dma("pool", c.W2S[cg * NFG + fg].rearrange("p (k f) -> p k f", k=4),
                        w2v[:, fg * 4:(fg + 1) * 4, cg * 512:(cg + 1) * 512], writes=[c.W2Sk[cg * NFG + fg]])
    return conv


def phase_norm(c, gi, add_mix, from_input, final=False):
    p, nc = c.p, c.nc
    with contextlib.ExitStack() as ph:
        gB = p.sb([128, D], F32, ph)
        p.dma("sp", gB.t[:], c.gains[gi:gi + 1, :].broadcast_to([128, D]), writes=[gB])
        NBUF = 3
        xt = [p.sb([128, D], F32, ph) for _ in range(NBUF)]
        mt = [p.sb([128, D], F32, ph) for _ in range(NBUF)]
        junk = [p.sb([128, D], BF16, ph) for _ in range(NBUF)]
        st_ = [p.sb([128, 4], F32, ph) for _ in range(NBUF)]
        if final:
            ob = [p.sb([128, D], F32, ph) for _ in range(NBUF)]
        else:
            hb = [p.sb([128, D], BF16, ph) for _ in range(NBUF)]
            hT = [p.sb([128, NDC, 128], BF16, ph) for _ in range(NBUF)]
        src = c.x_in if from_input else c.X

        def loads(t):
            b = t % NBUF
            rows = slice(t * 128, (t + 1) * 128)
            p.dma("sp", xt[b].t[:], src[rows, :], reads=[] if from_input else c.Xk[t], writes=[xt[b]])
            if add_mix:
                p.dma("sp", mt[b].t[:], c.MIX[rows, :], reads=[c.MIXk[t]], writes=[mt[b]])
        loads(0)
        if c.NT > 1:
            loads(1)
        for t in range(c.NT):
            b = t % NBUF
            X_, M_, J_, S_ = xt[b], mt[b], junk[b], st_[b]
            rows = slice(t * 128, (t + 1) * 128)
            if t + 2 < c.NT:
                loads(t + 2)
            if add_mix:
                p.op("dve", lambda e: e.tensor_tensor(out=X_.t[:], in0=X_.t[:], in1=M_.t[:], op=ALU.add),
                     reads=[X_, M_], writes=[X_])
            if (add_mix or from_input) and not final:
                p.dma("act", c.X[rows, :], X_.t[:], reads=[X_], writes=c.Xk[t])
            p.op("act", lambda e: e.activation(out=J_.t[:], in_=X_.t[:], func=AF.Square, accum_out=S_.t[:, 0:1]),
                 reads=[X_], writes=[J_, S_])
            p.op("dve", lambda e: e.tensor_scalar(out=S_.t[:, 1:2], in0=S_.t[:, 0:1], scalar1=1.0 / D, scalar2=EPS,
                                                  op0=ALU.mult, op1=ALU.add), reads=[S_], writes=[S_])
            p.op("act", lambda e: e.activation(out=S_.t[:, 2:3], in_=S_.t[:, 1:2], func=AF.Sqrt), reads=[S_], writes=[S_])
            p.op("dve", lambda e: e.reciprocal(out=S_.t[:, 3:4], in_=S_.t[:, 2:3]), reads=[S_], writes=[S_])
            if final:
                O_ = ob[b]
                p.op("dve", lambda e: e.scalar_tensor_tensor(out=O_.t[:], in0=X_.t[:], scalar=S_.t[:, 3:4], in1=gB.t[:],
                                                             op0=ALU.mult, op1=ALU.mult), reads=[X_, S_, gB], writes=[O_])
                p.dma("act", c.y[rows, :], O_.t[:], reads=[O_], writes=[c.yk[t]])
                continue
            H_, T_ = hb[b], hT[b]
            p.op("dve", lambda e: e.scalar_tensor_tensor(out=H_.t[:], in0=X_.t[:], scalar=S_.t[:, 3:4], in1=gB.t[:],
                                                         op0=ALU.mult, op1=ALU.mult), reads=[X_, S_, gB], writes=[H_])
            for half in range(2):
                P_ = c.psb[(2 * t + half) % 2]
                pv = P_.t[:].bitcast(BF16)
                for j in range(8):
                    dc = half * 8 + j
                    p.op("pe", lambda e: e.transpose(out=pv[:, j * 128:(j + 1) * 128],
                                                     in_=H_.t[:, dc * 128:(dc + 1) * 128], identity=c.ident.t[:]),
                         reads=[H_, c.ident], writes=[P_], inc=(j == 7))
                p.op("act", lambda e: e.activation(out=T_.t[:, half * 8:(half + 1) * 8, :],
                                                   in_=pv.rearrange("p (j q) -> p j q", j=8), func=AF.Copy),
                     reads=[P_], writes=[T_])
            p.dma("sp", c.HT[:, :, rows].rearrange("c p t -> p c t"), T_.t[:], reads=[T_], writes=[c.HTk[t]])
        p.barrier()


def phase_mlp(c, L):
    p, nc = c.p, c.nc
    FF = c.FF
    NF = FF // 128
    NFG = FF // 512
    w1v = L["w1"].rearrange("(k p) f -> p k f", p=128)
    w2v = L["w2"].rearrange("(f p) d -> p f d", p=128)
    with contextlib.ExitStack() as ph:
        hT = [p.sb([128, NDC, 512], BF16, ph) for _ in range(2)]
        aT = p.sb([128, NF, 512], BF16, ph)
        w1g = [p.sb([128, NDC, 512], BF16, ph) for _ in range(2)]
        w2g = [p.sb([128, 4, 512], BF16, ph) for _ in range(3)]
        rt = [p.sb([128, 512], F32, ph) for _ in range(2)]
        mo = [p.sb([128, 512], F32, ph) for _ in range(4)]
        xpb = [p.sb([128, 512], F32, ph) for _ in range(8)]
        nxp = 0
        aTk = [Tok() for _ in range(NFG)]
        nw1 = nw2 = nr = nm = 0
        L["conv"](0, 1)
        for blk in range(c.NB):
            H_ = hT[blk % 2]
            cols = slice(blk * 512, (blk + 1) * 512)
            p.dma("sp", H_.t[:], c.HT[:, :, cols].rearrange("c p t -> p c t"),
                  reads=[c.HTk[4 * blk + i] for i in range(4)], writes=[H_])
            for fg in range(NFG):
                W_ = w1g[nw1 % 2]
                nw1 += 1
                p.dma("pool", W_.t[:].rearrange("p k f -> p (k f)"), c.W1S[fg], reads=[c.W1Sk[fg]], writes=[W_])
                for fi in range(4):
                    f = fg * 4 + fi
                    P_ = c.psb[4 + (f % 2)]
                    for k in range(NDC):
                        p.op("pe", lambda e: e.matmul(P_.t[:], lhsT=W_.t[:, k, fi * 128:(fi + 1) * 128], rhs=H_.t[:, k, :],
                                                      start=(k == 0), stop=(k == NDC - 1)),
                             reads=[W_, H_], writes=[P_], inc=(k == NDC - 1))
                    R_ = rt[nr % 2]
                    nr += 1
                    p.op("act", lambda e: e.activation(out=R_.t[:], in_=P_.t[:], func=AF.Relu), reads=[P_], writes=[R_])
                    p.op("dve", lambda e: e.tensor_tensor(out=aT.t[:, f, :], in0=R_.t[:], in1=R_.t[:], op=ALU.mult),
                         reads=[R_], writes=[aTk[fg]])
            for cg in range(4):
                XP = []
                for tt in range(4):
                    t = 4 * blk + tt
                    XP_ = xpb[nxp % 8]
                    nxp += 1
                    p.dma("sp", XP_.t[:], c.X[t * 128:(t + 1) * 128, cg * 512:(cg + 1) * 512], reads=[c.Xk[t][cg]], writes=[XP_])
                    XP.append(XP_)
                for fg in range(NFG):
                    W_ = w2g[nw2 % 3]
                    nw2 += 1
                    p.dma("pool", W_.t[:].rearrange("p k f -> p (k f)"), c.W2S[cg * NFG + fg],
                          reads=[c.W2Sk[cg * NFG + fg]], writes=[W_])
                    for fi in range(4):
                        f = fg * 4 + fi
                        for tt in range(4):
                            P_ = c.psb[tt]
                            p.op("pe", lambda e: e.matmul(P_.t[:], lhsT=aT.t[:, f, tt * 128:(tt + 1) * 128], rhs=W_.t[:, fi, :],
                                                          start=(f == 0), stop=(f == NF - 1)),
                                 reads=[W_, aTk[fg]], writes=[P_], inc=(f == NF - 1 or (fi == 3 and tt == 3)))
                for tt in range(4):
                    M_ = mo[nm % 4]
                    nm += 1
                    P_ = c.psb[tt]
                    t = 4 * blk + tt
                    p.op("dve", lambda e: e.tensor_tensor(out=M_.t[:], in0=P_.t[:], in1=XP[tt].t[:], op=ALU.add),
                         reads=[P_, XP[tt]], writes=[M_])
                    p.dma("sp", c.X[t * 128:(t + 1) * 128, cg * 512:(cg + 1) * 512], M_.t[:], reads=[M_],
                          writes=[c.Xk[t][cg]])
        p.barrier()


def load_cols(p, ph, vec_ap, n, scale=None):
    t = p.sb([128, n], F32, ph)
    p.dma("sp", t.t[:], vec_ap.rearrange("(c p) -> p c", p=128), writes=[t], allow_slow_non_contiguous=True)
    if scale is not None:
        p.op("dve", lambda e: e.tensor_scalar(out=t.t[:], in0=t.t[:], scalar1=float(scale), scalar2=None, op0=ALU.mult),
             reads=[t], writes=[t])
    return t


def make_ones(c, ph):
    p = c.p
    o32 = p.sb([128, 128], F32, ph)
    ob = p.sb([128, 128], BF16, ph)
    p.op("dve", lambda e: e.memset(o32.t[:], 1.0), writes=[o32])
    p.op("dve", lambda e: e.tensor_copy(out=ob.t[:], in_=o32.t[:]), reads=[o32], writes=[ob])
    return ob


def phase_attn(c, L):
    p, nc = c.p, c.nc
    S, NT, NB = c.S, c.NT, c.NB
    NH = L["NH"]
    lam_init = L["lambda_init"]
    wq = L["wq"].rearrange("(k p) f -> p k f", p=128)
    wk = L["wk"].rearrange("(k p) f -> p k f", p=128)
    wv = L["wv"].rearrange("(k p) f -> p k f", p=128)
    with contextlib.ExitStack() as ph:
        ones = make_ones(c, ph)
        lp = p.sb([128, 4, 128], F32, ph)
        p.dma("sp", lp.t[:].rearrange("p a b -> p (a b)"),
              L["lam"].rearrange("a b -> (a b)").rearrange("(o n) -> o n", o=1).broadcast_to([128, 512]), writes=[lp])
        lt = p.sb([128, 8], F32, ph)
        pr = p.sb([128, 2, 128], F32, ph)
        p.op("dve", lambda e: e.tensor_tensor(out=pr.t[:, 0, :], in0=lp.t[:, 0, :], in1=lp.t[:, 1, :], op=ALU.mult),
             reads=[lp], writes=[pr])
        p.op("dve", lambda e: e.tensor_tensor(out=pr.t[:, 1, :], in0=lp.t[:, 2, :], in1=lp.t[:, 3, :], op=ALU.mult),
             reads=[lp, pr], writes=[pr])
        p.op("dve", lambda e: e.tensor_reduce(out=lt.t[:, 0:2], in_=pr.t[:], axis=AX.X, op=ALU.add), reads=[pr], writes=[lt])
        p.op("act", lambda e: e.activation(out=lt.t[:, 2:4], in_=lt.t[:, 0:2], func=AF.Exp), reads=[lt], writes=[lt])
        p.op("dve", lambda e: e.tensor_tensor(out=lt.t[:, 4:5], in0=lt.t[:, 3:4], in1=lt.t[:, 2:3], op=ALU.subtract),
             reads=[lt], writes=[lt])
        p.op("dve", lambda e: e.tensor_scalar(out=lt.t[:, 5:6], in0=lt.t[:, 4:5], scalar1=-float(lam_init), scalar2=None,
                                              op0=ALU.add), reads=[lt], writes=[lt])
        nlam = lt.t[:, 5:6]
        sg = load_cols(p, ph, L["subln"], 2, scale=1.0 - lam_init)
        QT = p.sb([128, 2, S], BF16, ph)
        KT = p.sb([128, 2, S], BF16, ph)
        V = p.sb([128, NT, 256], BF16, ph)
        QTk = [Tok() for _ in range(NB)]
        KTk = [Tok() for _ in range(NB)]
        Vk = [Tok() for _ in range(NB)]
        wqb = [p.sb([128, NDC, 256], BF16, ph) for _ in range(2)]
        wkb = [p.sb([128, NDC, 256], BF16, ph) for _ in range(2)]
        wvb = [p.sb([128, NDC, 256], BF16, ph) for _ in range(2)]
        hT = [p.sb([128, NDC, 512], BF16, ph) for _ in range(2)]
        PT = [p.sb([128, 512], BF16, ph) for _ in range(3)]
        rden = [p.sb([128, 512], F32, ph) for _ in range(2)]
        om = [p.sb([128, 2, 512], F32, ph) for _ in range(2)]
        o_ = p.sb([128, 2, 512], F32, ph)
        sq = p.sb([128, 2, 512], BF16, ph)
        rs = p.sb([128, 512], F32, ph)
        obf = [p.sb([128, 2, 512], BF16, ph) for _ in range(2)]
        psb = c.psb
        nh = npt = nob = 0

        def load_w(h):
            b = h % 2
            p.dma("pool", wqb[b].t[:], wq[:, :, h * 256:(h + 1) * 256], writes=[wqb[b]])
            p.dma("pool", wkb[b].t[:], wk[:, :, h * 256:(h + 1) * 256], writes=[wkb[b]])
            p.dma("pool", wvb[b].t[:], wv[:, :, h * 256:(h + 1) * 256], writes=[wvb[b]])
        load_w(0)
        for h in range(NH):
            b = h % 2
            if h + 1 < NH:
                load_w(h + 1)
            L["conv"](h, NH)
            nev = 0
            for blk in range(NB):
                H_ = hT[nh % 2]
                nh += 1
                cols = slice(blk * 512, (blk + 1) * 512)
                p.dma("sp", H_.t[:], c.HT[:, :, cols].rearrange("c p t -> p c t"),
                      reads=[c.HTk[4 * blk + i] for i in range(4)], writes=[H_])
                for which, W_, dst, dk, scl in ((0, wqb[b], QT, QTk, 128.0 ** -0.5), (1, wkb[b], KT, KTk, 1.0)):
                    for m in range(2):
                        P_ = psb[nev % 2]
                        nev += 1
                        for k in range(NDC):
                            p.op("pe", lambda e: e.matmul(P_.t[:], lhsT=W_.t[:, k, m * 128:(m + 1) * 128], rhs=H_.t[:, k, :],
                                                          start=(k == 0), stop=(k == NDC - 1)),
                                 reads=[W_, H_], writes=[P_], inc=(k == NDC - 1))
                        p.op("act", lambda e: e.activation(out=dst.t[:, m, cols], in_=P_.t[:], func=AF.Copy, scale=scl),
                             reads=[P_], writes=[dk[blk]])
                for tt in range(4):
                    P_ = psb[nev % 2]
                    nev += 1
                    W_ = wvb[b]
                    for k in range(NDC):
                        p.op("pe", lambda e: e.matmul(P_.t[:, 0:256], lhsT=H_.t[:, k, tt * 128:(tt + 1) * 128], rhs=W_.t[:, k, :],
                                                      start=(k == 0), stop=(k == NDC - 1)),
                             reads=[W_, H_], writes=[P_], inc=(k == NDC - 1))
                    p.op("dve", lambda e: e.tensor_copy(out=V.t[:, 4 * blk + tt, :], in_=P_.t[:, 0:256]),
                         reads=[P_], writes=[Vk[blk]])
            for qb in range(NB):
                nkt = 4 * qb + 4
                for m in range(2):
                    psO = [psb[2 + 3 * m], psb[3 + 3 * m]]
                    psD = psb[4 + 3 * m]
                    def emit_s(kt):
                        nonlocal npt
                        j = kt - 4 * qb
                        c0 = 128 * j if j >= 0 else 0
                        PS_ = psb[kt % 2]
                        p.op("pe", lambda e: e.matmul(PS_.t[:, c0:512], lhsT=KT.t[:, m, kt * 128:(kt + 1) * 128],
                                                      rhs=QT.t[:, m, qb * 512 + c0:(qb + 1) * 512], start=True, stop=True),
                             reads=[KTk[kt // 4], QTk[qb]], writes=[PS_])
                        T_ = PT[npt % 3]
                        npt += 1
                        p.op("act", lambda e: e.activation(out=T_.t[:, c0:512], in_=PS_.t[:, c0:512], func=AF.Exp),
                             reads=[PS_], writes=[T_])
                        if j >= 0:
                            p.op("dve", lambda e: e.memset(T_.t[64:128, c0:c0 + 64], 0.0), writes=[T_])
                        return kt, T_, c0

                    def emit_pv(kt, T_, c0):
                        last = (kt == nkt - 1)
                        for cc in range(2):
                            p.op("pe", lambda e: e.matmul(psO[cc].t[:, c0:512], lhsT=V.t[:, kt, cc * 128:(cc + 1) * 128],
                                                          rhs=T_.t[:, c0:512], start=(kt == 0), stop=last),
                                 reads=[Vk[kt // 4], T_], writes=[psO[cc]], inc=last)
                        p.op("pe", lambda e: e.matmul(psD.t[:, c0:512], lhsT=ones.t[:], rhs=T_.t[:, c0:512],
                                                      start=(kt == 0), stop=last),
                             reads=[ones, T_], writes=[psD], inc=True)
                    prev = None
                    for kt in range(nkt):
                        cur = emit_s(kt)
                        if prev is not None:
                            emit_pv(*prev)
                        prev = cur
                    emit_pv(*prev)
                    R_ = rden[m]
                    p.op("dve", lambda e: e.reciprocal(out=R_.t[:], in_=psD.t[:]), reads=[psD], writes=[R_])
                    for cc in range(2):
                        p.op("dve", lambda e: e.tensor_tensor(out=om[m].t[:, cc, :], in0=psO[cc].t[:], in1=R_.t[:], op=ALU.mult),
                             reads=[psO[cc], R_], writes=[om[m]])
                p.op("dve", lambda e: e.scalar_tensor_tensor(out=o_.t[:], in0=om[1].t[:], scalar=nlam, in1=om[0].t[:],
                                                             op0=ALU.mult, op1=ALU.add), reads=[om[0], om[1], lt], writes=[o_])
                p.op("act", lambda e: e.activation(out=sq.t[:], in_=o_.t[:], func=AF.Square), reads=[o_], writes=[sq])
                PN_ = psb[0]
                for cc in range(2):
                    p.op("pe", lambda e: e.matmul(PN_.t[:], lhsT=ones.t[:], rhs=sq.t[:, cc, :], start=(cc == 0), stop=(cc == 1)),
                         reads=[ones, sq], writes=[PN_], inc=(cc == 1))
                p.op("dve", lambda e: e.tensor_scalar(out=rs.t[:], in0=PN_.t[:], scalar1=1.0 / 256.0, scalar2=EPS,
                                                      op0=ALU.mult, op1=ALU.add), reads=[PN_], writes=[rs])
                p.op("act", lambda e: e.activation(out=rs.t[:], in_=rs.t[:], func=AF.Sqrt), reads=[rs], writes=[rs])
                p.op("dve", lambda e: e.reciprocal(out=rs.t[:], in_=rs.t[:]), reads=[rs], writes=[rs])
                OB_ = obf[nob % 2]
                nob += 1
                for cc in range(2):
                    p.op("dve", lambda e: e.scalar_tensor_tensor(out=OB_.t[:, cc, :], in0=o_.t[:, cc, :], scalar=sg.t[:, cc:cc + 1],
                                                                 in1=rs.t[:], op0=ALU.mult, op1=ALU.mult),
                         reads=[o_, sg, rs], writes=[OB_])
                p.dma("sp", c.OT[2 * h:2 * h + 2, :, qb * 512:(qb + 1) * 512].rearrange("c p t -> p c t"), OB_.t[:],
                      reads=[OB_], writes=[c.OTk[4 * qb + i] for i in range(4)])
        p.barrier()


def phase_outproj(c, wo_ap, NFC):
    p, nc = c.p, c.nc
    wov = wo_ap.rearrange("(f p) d -> p f d", p=128)
    with contextlib.ExitStack() as ph:
        wo = [p.sb([128, NFC, 512], BF16, ph) for _ in range(2)]
        oT = [p.sb([128, NFC, 512], BF16, ph) for _ in range(3)]
        mo = [p.sb([128, 512], F32, ph) for _ in range(4)]
        items = [(cg, blk) for cg in range(4) for blk in range(c.NB)]

        xpb = [p.sb([128, 512], F32, ph) for _ in range(12)]

        def load(i):
            cg, blk = items[i]
            p.dma("sp", oT[i % 3].t[:], c.OT[0:NFC, :, blk * 512:(blk + 1) * 512].rearrange("c p t -> p c t"),
                  reads=[c.OTk[4 * blk + j] for j in range(4)], writes=[oT[i % 3]])
            for tt in range(4):
                t = 4 * blk + tt
                XP_ = xpb[(i % 3) * 4 + tt]
                p.dma("sp", XP_.t[:], c.X[t * 128:(t + 1) * 128, cg * 512:(cg + 1) * 512], reads=[c.Xk[t][cg]], writes=[XP_])

        def loadw(cg):
            W_ = wo[cg % 2]
            for f in range(0, NFC, 8):
                p.dma("pool", W_.t[:, f:f + 8, :], wov[:, f:f + 8, cg * 512:(cg + 1) * 512], writes=[W_])
        loadw(0)
        load(0)
        load(1)
        n = 0
        for i, (cg, blk) in enumerate(items):
            if blk == 0 and cg + 1 < 4:
                loadw(cg + 1)
            if i + 2 < len(items):
                load(i + 2)
            W_ = wo[cg % 2]
            O_ = oT[i % 3]
            for tt in range(4):
                t = 4 * blk + tt
                M_ = mo[n % 4]
                P_ = c.psb[n % 4]
                n += 1
                for f in range(NFC):
                    p.op("pe", lambda e: e.matmul(P_.t[:], lhsT=O_.t[:, f, tt * 128:(tt + 1) * 128], rhs=W_.t[:, f, :],
                                                  start=(f == 0), stop=(f == NFC - 1)),
                         reads=[O_, W_], writes=[P_], inc=(f == NFC - 1))
                XP_ = xpb[(i % 3) * 4 + tt]
                p.op("dve", lambda e: e.tensor_tensor(out=M_.t[:], in0=P_.t[:], in1=XP_.t[:], op=ALU.add), reads=[P_, XP_], writes=[M_])
                p.dma("sp", c.X[t * 128:(t + 1) * 128, cg * 512:(cg + 1) * 512], M_.t[:], reads=[M_], writes=[c.Xk[t][cg]])
        p.barrier()


RET_SC = 512


def ret_consts(S, NH_all=8):
    half = 128
    inv = (1.0 / (np.float32(10000.0) ** np.linspace(0.0, 1.0, half, dtype=np.float32))).astype(np.float32)
    pos = np.arange(S, dtype=np.float32)
    ang = (pos[None, :] * inv[:, None]).astype(np.float32)
    cosT = np.cos(ang).astype(np.float32)
    sinT = np.sin(ang).astype(np.float32)
    lg = np.log(1.0 - np.exp2(-5.0 - np.arange(NH_all, dtype=np.float64)))
    scale = 256.0 ** -0.5
    ql = np.arange(RET_SC)
    dtab = np.zeros((NH_all, 4, 128, RET_SC), np.float32)
    for kt in range(4):
        km = 128 * kt + np.arange(128)
        dist = np.abs(ql[None, :] - km[:, None])
        mask = (km[:, None] // 64) <= (ql[None, :] // 64)
        for h in range(NH_all):
            dtab[h, kt] = (scale * np.exp(lg[h] * dist) * mask).astype(np.float32)
    qdec = (scale * np.exp(lg[:, None] * (ql[None, :] + 1.0))).astype(np.float32)
    ml = np.arange(RET_SC)
    kdec = np.exp(lg[:, None] * (RET_SC - 1.0 - ml[None, :])).astype(np.float32)
    kdecT = np.ascontiguousarray(kdec.reshape(NH_all, 4, 128).transpose(0, 2, 1))
    g512 = np.exp(lg * RET_SC)
    return cosT, sinT, dtab, qdec, kdecT, [float(v) for v in g512]


def phase_ret(c, L):
    p, nc = c.p, c.nc
    S, NT, NB = c.S, c.NT, c.NB
    NH = L["NH"]
    wq = L["wq"].rearrange("(k p) f -> p k f", p=128)
    wk = L["wk"].rearrange("(k p) f -> p k f", p=128)
    wv = L["wv"].rearrange("(k p) f -> p k f", p=128)
    wg = L["wg"].rearrange("(k p) f -> p k f", p=128)
    psb = c.psb
    with contextlib.ExitStack() as ph:
        ones = make_ones(c, ph)
        cosT = p.sb([128, S], F32, ph)
        sinT = p.sb([128, S], F32, ph)
        p.dma("sp", cosT.t[:], L["cos"], writes=[cosT])
        p.dma("sp", sinT.t[:], L["sin"], writes=[sinT])
        wqb = p.sb([128, NDC, 256], BF16, ph)
        wkb = p.sb([128, NDC, 256], BF16, ph)
        wvb = p.sb([128, NDC, 512], BF16, ph)
        wgb = p.sb([128, NDC, 512], BF16, ph)
        dtab = p.sb([128, 4, 512], F32, ph)
        qdec = p.sb([128, 512], F32, ph)
        kdec = p.sb([128, 4], F32, ph)
        hT = [p.sb([128, NDC, 512], BF16, ph) for _ in range(2)]
        ra = p.sb([128, 2, 512], F32, ph)
        rb = p.sb([128, 2, 512], F32, ph)
        rq = p.sb([128, 2, 512], F32, ph)
        QT = p.sb([128, 2, 512], BF16, ph)
        QD = p.sb([128, 2, 512], BF16, ph)
        KT = p.sb([128, 2, 512], BF16, ph)
        Kd = p.sb([128, 4, 256], BF16, ph)
        V = p.sb([128, 4, 512], BF16, ph)
        G = p.sb([128, 4, 512], F32, ph)
        R32 = p.sb([128, 2, 512], F32, ph)
        Rb = p.sb([128, 2, 512], BF16, ph)
        PT = [p.sb([128, 512], BF16, ph) for _ in range(2)]
        o_ = p.sb([128, 4, 512], F32, ph)
        sq = p.sb([128, 4, 512], BF16, ph)
        rs = p.sb([128, 512], F32, ph)
        og = [p.sb([128, 4, 512], BF16, ph) for _ in range(2)]
        nh = nog = 0
        for h in range(NH):
            hg = L["head0"] + h
            p.dma("pool", wqb.t[:], wq[:, :, h * 256:(h + 1) * 256], writes=[wqb])
            p.dma("pool", wkb.t[:], wk[:, :, h * 256:(h + 1) * 256], writes=[wkb])
            p.dma("pool", wvb.t[:], wv[:, :, h * 512:(h + 1) * 512], writes=[wvb])
            p.dma("pool", wgb.t[:], wg[:, :, h * 512:(h + 1) * 512], writes=[wgb])
            L["conv"](h, NH)
            p.dma("sp", dtab.t[:], L["dtab"][hg].rearrange("k p q -> p k q"), writes=[dtab])
            p.dma("sp", qdec.t[:], L["qdec"][hg:hg + 1, :].broadcast_to([128, 512]), writes=[qdec])
            p.dma("sp", kdec.t[:], L["kdecT"][hg], writes=[kdec])
            g512 = L["g512"][hg]
            for sc in range(NB):
                H_ = hT[nh % 2]
                nh += 1
                cols = slice(sc * 512, (sc + 1) * 512)
                p.dma("sp", H_.t[:], c.HT[:, :, cols].rearrange("c p t -> p c t"),
                      reads=[c.HTk[4 * sc + i] for i in range(4)], writes=[H_])
                for which, W_ in ((0, wqb), (1, wkb)):
                    for m in range(2):
                        P_ = psb[m]
                        for k in range(NDC):
                            p.op("pe", lambda e: e.matmul(P_.t[:], lhsT=W_.t[:, k, m * 128:(m + 1) * 128], rhs=H_.t[:, k, :],
                                                          start=(k == 0), stop=(k == NDC - 1)),
                                 reads=[W_, H_], writes=[P_], inc=(k == NDC - 1))
                    for m in range(2):
                        p.op("dve", lambda e: e.tensor_tensor(out=ra.t[:, m, :], in0=psb[m].t[:], in1=cosT.t[:, cols], op=ALU.mult),
                             reads=[psb[m], cosT], writes=[ra])
                        p.op("dve", lambda e: e.tensor_tensor(out=rb.t[:, 1 - m, :], in0=psb[m].t[:], in1=sinT.t[:, cols], op=ALU.mult),
                             reads=[psb[m], sinT], writes=[rb])
                    if which == 0:
                        p.op("dve", lambda e: e.tensor_tensor(out=rq.t[:, 0, :], in0=ra.t[:, 0, :], in1=rb.t[:, 0, :], op=ALU.subtract),
                             reads=[ra, rb], writes=[rq])
                        p.op("dve", lambda e: e.tensor_tensor(out=rq.t[:, 1, :], in0=ra.t[:, 1, :], in1=rb.t[:, 1, :], op=ALU.add),
                             reads=[ra, rb], writes=[rq])
                        p.op("act", lambda e: e.activation(out=QT.t[:], in_=rq.t[:], func=AF.Copy), reads=[rq], writes=[QT])
                        for m in range(2):
                            p.op("dve", lambda e: e.tensor_tensor(out=QD.t[:, m, :], in0=rq.t[:, m, :], in1=qdec.t[:], op=ALU.mult),
                                 reads=[rq, qdec], writes=[QD])
                    else:
                        p.op("dve", lambda e: e.tensor_tensor(out=KT.t[:, 0, :], in0=ra.t[:, 0, :], in1=rb.t[:, 0, :], op=ALU.subtract),
                             reads=[ra, rb], writes=[KT])
                        p.op("dve", lambda e: e.tensor_tensor(out=KT.t[:, 1, :], in0=ra.t[:, 1, :], in1=rb.t[:, 1, :], op=ALU.add),
                             reads=[ra, rb], writes=[KT])
                for tt in range(4):
                    P_ = psb[tt % 2]
                    for k in range(NDC):
                        p.op("pe", lambda e: e.matmul(P_.t[:], lhsT=H_.t[:, k, tt * 128:(tt + 1) * 128], rhs=wvb.t[:, k, :],
                                                      start=(k == 0), stop=(k == NDC - 1)),
                             reads=[wvb, H_], writes=[P_], inc=(k == NDC - 1))
                    p.op("act", lambda e: e.activation(out=V.t[:, tt, :], in_=P_.t[:], func=AF.Copy), reads=[P_], writes=[V])
                for gc in range(4):
                    P_ = psb[gc % 2]
                    for k in range(NDC):
                        p.op("pe", lambda e: e.matmul(P_.t[:], lhsT=wgb.t[:, k, gc * 128:(gc + 1) * 128], rhs=H_.t[:, k, :],
                                                      start=(k == 0), stop=(k == NDC - 1)),
                             reads=[wgb, H_], writes=[P_], inc=(k == NDC - 1))
                    p.op("act", lambda e: e.activation(out=G.t[:, gc, :], in_=P_.t[:], func=AF.Silu), reads=[P_], writes=[G])
                PK_ = psb[0]
                pkv = PK_.t[:].bitcast(BF16)
                for tt in range(4):
                    for cq in range(2):
                        i8 = tt * 2 + cq
                        p.op("pe", lambda e: e.transpose(out=pkv[:, i8 * 128:(i8 + 1) * 128], in_=KT.t[:, cq, tt * 128:(tt + 1) * 128],
                                                         identity=c.ident.t[:]),
                             reads=[KT, c.ident], writes=[PK_], inc=(i8 == 7))
                for tt in range(4):
                    p.op("dve", lambda e: e.tensor_scalar(out=Kd.t[:, tt, :], in0=pkv[:, tt * 256:(tt + 1) * 256],
                                                          scalar1=kdec.t[:, tt:tt + 1], scalar2=None, op0=ALU.mult),
                         reads=[PK_, kdec], writes=[Kd])
                psO = [psb[4 + ec] for ec in range(4)]
                started = [False] * 4
                if sc > 0:
                    for ec in range(4):
                        for dc in range(2):
                            p.op("pe", lambda e: e.matmul(psO[ec].t[:], lhsT=Rb.t[:, dc, ec * 128:(ec + 1) * 128], rhs=QD.t[:, dc, :],
                                                          start=(dc == 0), stop=False),
                                 reads=[Rb, QD], writes=[psO[ec]], inc=(dc == 1))
                        started[ec] = True
                for kt in range(4):
                    c0 = 128 * kt
                    PS_ = psb[2 + kt % 2]
                    for cq in range(2):
                        p.op("pe", lambda e: e.matmul(PS_.t[:, c0:512], lhsT=KT.t[:, cq, kt * 128:(kt + 1) * 128],
                                                      rhs=QT.t[:, cq, c0:512], start=(cq == 0), stop=(cq == 1)),
                             reads=[KT, QT], writes=[PS_], inc=(cq == 1))
                    T_ = PT[kt % 2]
                    p.op("dve", lambda e: e.tensor_tensor(out=T_.t[:, c0:512], in0=PS_.t[:, c0:512], in1=dtab.t[:, kt, c0:512],
                                                          op=ALU.mult), reads=[PS_, dtab], writes=[T_])
                    for ec in range(4):
                        p.op("pe", lambda e: e.matmul(psO[ec].t[:, c0:512], lhsT=V.t[:, kt, ec * 128:(ec + 1) * 128],
                                                      rhs=T_.t[:, c0:512], start=(not started[ec]), stop=(kt == 3)),
                             reads=[V, T_], writes=[psO[ec]], inc=(ec == 3 or kt == 3))
                        started[ec] = True
                for ec in range(4):
                    p.op("act", lambda e: e.activation(out=o_.t[:, ec, :], in_=psO[ec].t[:], func=AF.Copy),
                         reads=[psO[ec]], writes=[o_])
                if sc < NB - 1:
                    for dc in range(2):
                        P_ = psb[dc]
                        for tt in range(4):
                            p.op("pe", lambda e: e.matmul(P_.t[:], lhsT=Kd.t[:, tt, dc * 128:(dc + 1) * 128], rhs=V.t[:, tt, :],
                                                          start=(tt == 0), stop=(tt == 3)),
                                 reads=[Kd, V], writes=[P_], inc=(tt == 3))
                        if sc == 0:
                            p.op("dve", lambda e: e.tensor_copy(out=R32.t[:, dc, :], in_=P_.t[:]), reads=[P_], writes=[R32])
                        else:
                            p.op("dve", lambda e: e.scalar_tensor_tensor(out=R32.t[:, dc, :], in0=R32.t[:, dc, :], scalar=float(g512),
                                                                         in1=P_.t[:], op0=ALU.mult, op1=ALU.add),
                                 reads=[R32, P_], writes=[R32])
                    p.op("act", lambda e: e.activation(out=Rb.t[:], in_=R32.t[:], func=AF.Copy), reads=[R32], writes=[Rb])
                p.op("act", lambda e: e.activation(out=sq.t[:], in_=o_.t[:], func=AF.Square), reads=[o_], writes=[sq])
                PN_ = psb[2]
                for ec in range(4):
                    p.op("pe", lambda e: e.matmul(PN_.t[:], lhsT=ones.t[:], rhs=sq.t[:, ec, :], start=(ec == 0), stop=(ec == 3)),
                         reads=[ones, sq], writes=[PN_], inc=(ec == 3))
                p.op("dve", lambda e: e.tensor_scalar(out=rs.t[:], in0=PN_.t[:], scalar1=1.0 / 512.0, scalar2=EPS,
                                                      op0=ALU.mult, op1=ALU.add), reads=[PN_], writes=[rs])
                p.op("act", lambda e: e.activation(out=rs.t[:], in_=rs.t[:], func=AF.Sqrt), reads=[rs], writes=[rs])
                p.op("dve", lambda e: e.reciprocal(out=rs.t[:], in_=rs.t[:]), reads=[rs], writes=[rs])
                OG_ = og[nog % 2]
                nog += 1
                for ec in range(4):
                    p.op("dve", lambda e: e.tensor_tensor(out=o_.t[:, ec, :], in0=o_.t[:, ec, :], in1=rs.t[:], op=ALU.mult),
                         reads=[o_, rs], writes=[o_])
                p.op("dve", lambda e: e.tensor_tensor(out=OG_.t[:], in0=o_.t[:], in1=G.t[:], op=ALU.mult),
                     reads=[o_, G], writes=[OG_])
                p.dma("sp", c.OT[4 * h:4 * h + 4, :, cols].rearrange("c p t -> p c t"), OG_.t[:],
                      reads=[OG_], writes=[c.OTk[4 * sc + i] for i in range(4)])
        p.barrier()


def range_reduce(p, out, in_, ti, tf, shape_ap=lambda t: t.t[:], pre_add=None):
    o, i_, a, b = shape_ap(out), shape_ap(in_), shape_ap(ti), shape_ap(tf)
    src = in_
    if pre_add is not None:
        p.op("dve", lambda e: e.tensor_scalar(out=o, in0=i_, scalar1=float(pre_add), scalar2=None, op0=ALU.add),
             reads=[in_], writes=[out])
        i_ = o
        src = out
    p.op("dve", lambda e: e.tensor_scalar(out=a, in0=i_, scalar1=1.0 / TWO_PI, scalar2=None, op0=ALU.mult),
         reads=[src], writes=[ti])
    p.op("dve", lambda e: e.tensor_copy(out=b, in_=a), reads=[ti], writes=[tf])
    p.op("dve", lambda e: e.scalar_tensor_tensor(out=o, in0=b, scalar=-TWO_PI, in1=i_, op0=ALU.mult, op1=ALU.add),
         reads=[tf, src], writes=[out])
    p.op("dve", lambda e: e.tensor_scalar(out=b, in0=o, scalar1=math.pi, scalar2=-TWO_PI, op0=ALU.is_gt, op1=ALU.mult),
         reads=[out], writes=[tf])
    p.op("dve", lambda e: e.tensor_tensor(out=o, in0=o, in1=b, op=ALU.add), reads=[out, tf], writes=[out])
    p.op("dve", lambda e: e.tensor_scalar(out=b, in0=o, scalar1=-math.pi, scalar2=TWO_PI, op0=ALU.is_lt, op1=ALU.mult),
         reads=[out], writes=[tf])
    p.op("dve", lambda e: e.tensor_tensor(out=o, in0=o, in1=b, op=ALU.add), reads=[out, tf], writes=[out])


def phase_s5(c, L):
    p, nc = c.p, c.nc
    S, NT, NB = c.S, c.NT, c.NB
    NG = L["NG"]
    NK = NG // 8
    NS = NG // 2
    assert NG == 128
    psb = c.psb
    with contextlib.ExitStack() as ph:
        Bl = p.sb([128, NK, 2, 128], BF16, ph)
        BlO = p.sb([128, NK, 2, 128], BF16, ph)
        magst = p.sb([128, NS], F32, ph)
        thst = p.sb([128, NS], F32, ph)
        Cl = p.sb([128, NS, 2, 128], BF16, ph)
        J = p.sb([128, 512], F32, ph)
        dcol = p.sb([128, NK], F32, ph)
        pp = contextlib.ExitStack()
        idf = p.sb([128, 128], F32, pp)
        p.dma("sp", idf.t[:], c.ident_in, writes=[idf])
        are = p.sb([128, 64], F32, pp)
        aim = p.sb([128, 64], F32, pp)
        sc_ = p.sb([128, 4], F32, pp)
        p.dma("sp", are.t[:], L["a_re"], writes=[are])
        p.dma("sp", aim.t[:], L["a_im"], writes=[aim])
        p.dma("sp", sc_.t[:, 0:1], L["log_dt"].rearrange("(g o) -> g o", o=1), writes=[sc_])
        p.op("act", lambda e: e.activation(out=sc_.t[:, 1:2], in_=sc_.t[:, 0:1], func=AF.Exp), reads=[sc_], writes=[sc_])
        dt = sc_.t[:, 1:2]
        mag2 = p.sb([128, 2, 64], F32, pp)
        th2 = p.sb([128, 2, 64], F32, pp)
        tmp = p.sb([128, 64], F32, pp)
        ti = p.sb([128, 64], I32, pp)
        tf = p.sb([128, 64], F32, pp)
        thc = p.sb([128, 64], F32, pp)
        sn = p.sb([128, 64], F32, pp)
        cs = p.sb([128, 64], F32, pp)
        p.op("dve", lambda e: e.tensor_scalar(out=tmp.t[:], in0=are.t[:], scalar1=dt, scalar2=None, op0=ALU.mult),
             reads=[are, sc_], writes=[tmp])
        p.op("act", lambda e: e.activation(out=mag2.t[:, 0, :], in_=tmp.t[:], func=AF.Exp), reads=[tmp], writes=[mag2])
        p.op("act", lambda e: e.activation(out=mag2.t[:, 1, :], in_=tmp.t[:], func=AF.Exp), reads=[tmp], writes=[mag2])
        p.op("dve", lambda e: e.tensor_scalar(out=tmp.t[:], in0=aim.t[:], scalar1=dt, scalar2=None, op0=ALU.mult),
             reads=[aim, sc_, mag2], writes=[tmp])
        sl0 = lambda t: t.t[:, 0, :] if len(t.t.shape) == 3 else t.t[:]
        range_reduce(p, th2, tmp, ti, tf, shape_ap=sl0)
        p.op("dve", lambda e: e.tensor_copy(out=th2.t[:, 1, :], in_=th2.t[:, 0, :]), reads=[th2], writes=[th2])
        p.op("act", lambda e: e.activation(out=sn.t[:], in_=th2.t[:, 0, :], func=AF.Sin), reads=[th2], writes=[sn])
        range_reduce(p, thc, th2, ti, tf, shape_ap=sl0, pre_add=math.pi / 2)
        p.op("act", lambda e: e.activation(out=cs.t[:], in_=thc.t[:], func=AF.Sin), reads=[thc], writes=[cs])
        nr = p.sb([128, 64], F32, pp)
        ni = p.sb([128, 64], F32, pp)
        den = p.sb([128, 64], F32, pp)
        cre = p.sb([128, 64], F32, pp)
        cim = p.sb([128, 64], F32, pp)
        t2 = p.sb([128, 64], F32, pp)
        p.op("dve", lambda e: e.tensor_tensor(out=nr.t[:], in0=mag2.t[:, 0, :], in1=cs.t[:], op=ALU.mult), reads=[mag2, cs], writes=[nr])
        p.op("dve", lambda e: e.tensor_scalar(out=nr.t[:], in0=nr.t[:], scalar1=-1.0, scalar2=None, op0=ALU.add), reads=[nr], writes=[nr])
        p.op("dve", lambda e: e.tensor_tensor(out=ni.t[:], in0=mag2.t[:, 0, :], in1=sn.t[:], op=ALU.mult), reads=[mag2, sn], writes=[ni])
        p.op("dve", lambda e: e.tensor_tensor(out=den.t[:], in0=are.t[:], in1=are.t[:], op=ALU.mult), reads=[are], writes=[den])
        p.op("dve", lambda e: e.tensor_tensor(out=t2.t[:], in0=aim.t[:], in1=aim.t[:], op=ALU.mult), reads=[aim], writes=[t2])
        p.op("dve", lambda e: e.tensor_tensor(out=den.t[:], in0=den.t[:], in1=t2.t[:], op=ALU.add), reads=[den, t2], writes=[den])
        p.op("dve", lambda e: e.reciprocal(out=den.t[:], in_=den.t[:]), reads=[den], writes=[den])
        p.op("dve", lambda e: e.tensor_tensor(out=cre.t[:], in0=nr.t[:], in1=are.t[:], op=ALU.mult), reads=[nr, are], writes=[cre])
        p.op("dve", lambda e: e.tensor_tensor(out=t2.t[:], in0=ni.t[:], in1=aim.t[:], op=ALU.mult), reads=[ni, aim], writes=[t2])
        p.op("dve", lambda e: e.tensor_tensor(out=cre.t[:], in0=cre.t[:], in1=t2.t[:], op=ALU.add), reads=[cre, t2], writes=[cre])
        p.op("dve", lambda e: e.tensor_tensor(out=cre.t[:], in0=cre.t[:], in1=den.t[:], op=ALU.mult), reads=[cre, den], writes=[cre])
        p.op("dve", lambda e: e.tensor_tensor(out=cim.t[:], in0=ni.t[:], in1=are.t[:], op=ALU.mult), reads=[ni, are], writes=[cim])
        p.op("dve", lambda e: e.tensor_tensor(out=t2.t[:], in0=nr.t[:], in1=aim.t[:], op=ALU.mult), reads=[nr, aim], writes=[t2])
        p.op("dve", lambda e: e.tensor_tensor(out=cim.t[:], in0=cim.t[:], in1=t2.t[:], op=ALU.subtract), reads=[cim, t2], writes=[cim])
        p.op("dve", lambda e: e.tensor_tensor(out=cim.t[:], in0=cim.t[:], in1=den.t[:], op=ALU.mult), reads=[cim, den], writes=[cim])
        bre = p.sb([128, 64, 16], F32, pp)
        bim = p.sb([128, 64, 16], F32, pp)
        p.dma("sp", bre.t[:], L["b_re"], writes=[bre])
        p.dma("sp", bim.t[:], L["b_im"], writes=[bim])
        u1 = p.sb([128, 64, 16], F32, pp)
        u2 = p.sb([128, 64, 16], F32, pp)
        bb = p.sb([128, 2, 16, 64], BF16, pp)
        creB = cre.t[:].unsqueeze(2).broadcast_to([128, 64, 16])
        cimB = cim.t[:].unsqueeze(2).broadcast_to([128, 64, 16])
        p.op("dve", lambda e: e.tensor_tensor(out=u1.t[:], in0=bre.t[:], in1=creB, op=ALU.mult), reads=[bre, cre], writes=[u1])
        p.op("dve", lambda e: e.tensor_tensor(out=u2.t[:], in0=bim.t[:], in1=cimB, op=ALU.mult), reads=[bim, cim], writes=[u2])
        p.op("dve", lambda e: e.tensor_tensor(out=bb.t[:, 0, :, :].rearrange("g c q -> g q c"), in0=u1.t[:], in1=u2.t[:], op=ALU.subtract),
             reads=[u1, u2], writes=[bb])
        p.op("dve", lambda e: e.tensor_tensor(out=u1.t[:], in0=bim.t[:], in1=creB, op=ALU.mult), reads=[bim, cre, bb], writes=[u1])
        p.op("dve", lambda e: e.tensor_tensor(out=u2.t[:], in0=bre.t[:], in1=cimB, op=ALU.mult), reads=[bre, cim, bb], writes=[u2])
        p.op("dve", lambda e: e.tensor_tensor(out=bb.t[:, 1, :, :].rearrange("g c q -> g q c"), in0=u1.t[:], in1=u2.t[:], op=ALU.add),
             reads=[u1, u2], writes=[bb])
        bbk = Tok()
        p.dma("sp", c.BBS.rearrange("r g c q -> g r c q"), bb.t[:], reads=[bb], writes=[bbk])
        p.op("dve", lambda e: e.memset(Bl.t[:], 0.0), writes=[Bl])
        p.op("dve", lambda e: e.memset(BlO.t[:], 0.0), writes=[BlO])
        bbv = c.BBS.rearrange("r (k e) c q -> e c k r q", e=8)
        for g8 in range(8):
            for ri in range(2):
                p.dma("sp", Bl.t[16 * g8:16 * g8 + 16, :, ri, 64 * (g8 % 2):64 * (g8 % 2) + 64], bbv[g8][:, :, ri, :],
                      reads=[bbk], writes=[Bl])
                if (g8 // 2) % 2 == 1:
                    p.dma("sp", BlO.t[16 * g8:16 * g8 + 16, :, ri, 64 * (g8 % 2):64 * (g8 % 2) + 64], bbv[g8][:, :, ri, :],
                          reads=[bbk], writes=[BlO])
        for src, dst in ((mag2, magst), (th2, thst)):
            P_ = psb[0]
            p.op("pe", lambda e: e.transpose(out=P_.t[:, 0:128], in_=src.t[:].rearrange("g a q -> g (a q)"), identity=idf.t[:]),
                 reads=[src, idf], writes=[P_])
            pv = P_.t[:, 0:128].rearrange("p (s two) -> p s two", two=2)
            p.op("dve", lambda e: e.tensor_copy(out=dst.t[0:64, :], in_=pv[0:64, :, 0]), reads=[P_], writes=[dst])
            p.op("dve", lambda e: e.tensor_copy(out=dst.t[64:128, :], in_=pv[64:128, :, 1]), reads=[P_], writes=[dst])
        p.op("dve", lambda e: e.memset(Cl.t[:], 0.0), writes=[Cl])
        cd = p.sb([128, 16, 2, 64], F32, pp)
        for ri, src_ap in ((0, L["c_re"]), (1, L["c_im"])):
            for dup in range(2):
                p.dma("sp", cd.t[:, :, dup, :], src_ap, writes=[cd])
            sgn = 1.0 if ri == 0 else -1.0
            for q4 in range(4):
                P_ = psb[q4 % 2]
                for j in range(4):
                    cp = q4 * 4 + j
                    p.op("pe", lambda e: e.transpose(out=P_.t[:, j * 128:(j + 1) * 128],
                                                     in_=cd.t[:, cp, :, :].rearrange("g a q -> g (a q)"), identity=idf.t[:]),
                         reads=[cd, idf], writes=[P_], inc=(j == 3))
                pv = P_.t[:].rearrange("p (j s two) -> p s two j", j=4, two=2)
                clv = Cl.t[:].rearrange("p (k f) r n -> p k f r n", f=4)
                pv5 = P_.t[:].rearrange("p (j k f two) -> p k f two j", j=4, f=4, two=2)
                for s4 in range(4):
                    p.op("dve", lambda e: e.tensor_scalar(out=clv[0:64, :, s4, ri, 32 * s4 + q4 * 4:32 * s4 + q4 * 4 + 4],
                                                          in0=pv5[0:64, :, s4, 0, :], scalar1=sgn, scalar2=None, op0=ALU.mult),
                         reads=[P_], writes=[Cl])
                    p.op("dve", lambda e: e.tensor_scalar(out=clv[64:128, :, s4, ri, 32 * s4 + 16 + q4 * 4:32 * s4 + 16 + q4 * 4 + 4],
                                                          in0=pv5[64:128, :, s4, 1, :], scalar1=sgn, scalar2=None, op0=ALU.mult),
                         reads=[P_], writes=[Cl])
        p.dma("sp", dcol.t[:], L["d"].rearrange("(c p) -> p c", p=128), writes=[dcol], allow_slow_non_contiguous=True)
        p.dma("sp", J.t[:], L["iota"].broadcast_to([128, 512]), writes=[J])
        p.barrier()
        pp.close()
        hk = [p.sb([128, S], BF16, ph) for _ in range(2)]
        gk = [p.sb([128, S], BF16, ph) for _ in range(2)]
        cosT = [p.sb([128, 512], F32, ph) for _ in range(4)]
        sinT = [p.sb([128, 512], F32, ph) for _ in range(4)]
        rT = [p.sb([128, 512], F32, ph) for _ in range(4)]
        ang = p.sb([128, 512], F32, ph)
        angi = p.sb([128, 512], I32, ph)
        angf = p.sb([128, 512], F32, ph)
        ang2 = p.sb([128, 512], F32, ph)
        carry = [p.sb([128, 2], F32, ph) for _ in range(4)]
        d3 = [p.sb([128, 512], F32, ph) for _ in range(3)]
        d4 = [p.sb([128, 512], F32, ph) for _ in range(3)]
        m1 = [p.sb([128, 512], F32, ph) for _ in range(2)]
        m2 = [p.sb([128, 512], F32, ph) for _ in range(2)]
        d1 = [p.sb([128, 512], F32, ph) for _ in range(2)]
        d2 = [p.sb([128, 512], F32, ph) for _ in range(2)]
        mre = [p.sb([128, 512], F32, ph) for _ in range(2)]
        mim = [p.sb([128, 512], F32, ph) for _ in range(2)]
        wre = [p.sb([128, 512], F32, ph) for _ in range(2)]
        wim = [p.sb([128, 512], F32, ph) for _ in range(2)]
        xre = [p.sb([128, 512], F32, ph) for _ in range(3)]
        xim = [p.sb([128, 512], F32, ph) for _ in range(3)]
        xrb = [p.sb([128, 512], BF16, ph) for _ in range(3)]
        xib = [p.sb([128, 512], BF16, ph) for _ in range(3)]
        yv = p.sb([128, 512], F32, ph)
        y2 = p.sb([128, 512], F32, ph)
        sgm = p.sb([128, 512], F32, ph)
        n = 0
        for k in range(NK):
            HK_ = hk[k % 2]
            GK_ = gk[k % 2]
            p.dma("sp", HK_.t[:], c.HT[k, :, :], reads=c.HTk, writes=[HK_])
            L["conv"](k, NK)
            for s4 in range(4):
                s = 4 * k + s4
                phi = thst.t[:, s:s + 1]
                p.op("dve", lambda e: e.tensor_scalar(out=ang.t[:], in0=J.t[:], scalar1=phi, scalar2=None, op0=ALU.mult),
                     reads=[J, thst], writes=[ang])
                range_reduce(p, ang2, ang, angi, angf)
                p.op("act", lambda e: e.activation(out=sinT[s4].t[:], in_=ang2.t[:], func=AF.Sin), reads=[ang2], writes=[sinT[s4]])
                range_reduce(p, ang2, ang, angi, angf, pre_add=math.pi / 2)
                p.op("act", lambda e: e.activation(out=cosT[s4].t[:], in_=ang2.t[:], func=AF.Sin), reads=[ang2], writes=[cosT[s4]])
                p.op("dve", lambda e: e.tensor_scalar(out=rT[s4].t[:], in0=J.t[:], scalar1=0.0, scalar2=magst.t[:, s:s + 1],
                                                      op0=ALU.mult, op1=ALU.add), reads=[J, magst], writes=[rT[s4]])
            pend = []

            def tail(s, s4, b3, PY_):
                p.op("dve", lambda e: e.tensor_tensor(out=xim[b3].t[:], in0=d3[b3].t[:], in1=d4[b3].t[:], op=ALU.add), reads=[d3[b3], d4[b3]], writes=[xim[b3]])
                p.op("act", lambda e: e.activation(out=carry[s4].t[:, 0:1], in_=xre[b3].t[:, 511:512], func=AF.Copy), reads=[xre[b3]], writes=[carry[s4]])
                p.op("act", lambda e: e.activation(out=carry[s4].t[:, 1:2], in_=xim[b3].t[:, 511:512], func=AF.Copy), reads=[xim[b3]], writes=[carry[s4]])
                p.op("act", lambda e: e.activation(out=xrb[b3].t[:], in_=xre[b3].t[:], func=AF.Copy), reads=[xre[b3]], writes=[xrb[b3]])
                p.op("act", lambda e: e.activation(out=xib[b3].t[:], in_=xim[b3].t[:], func=AF.Copy), reads=[xim[b3]], writes=[xib[b3]])
                p.op("pe", lambda e: e.matmul(PY_.t[:], lhsT=Cl.t[:, s, 0, :], rhs=xrb[b3].t[:], start=(s4 == 0), stop=False),
                     reads=[Cl, xrb[b3]], writes=[PY_], inc=False)
                p.op("pe", lambda e: e.matmul(PY_.t[:], lhsT=Cl.t[:, s, 1, :], rhs=xib[b3].t[:], start=False, stop=(s4 == 3)),
                     reads=[Cl, xib[b3]], writes=[PY_])

            def evac(cols, PY_):
                p.op("dve", lambda e: e.scalar_tensor_tensor(out=yv.t[:], in0=HK_.t[:, cols], scalar=dcol.t[:, k:k + 1], in1=PY_.t[:],
                                                             op0=ALU.mult, op1=ALU.add), reads=[HK_, dcol, PY_], writes=[yv])
                p.op("act", lambda e: e.activation(out=y2.t[:], in_=yv.t[:], func=AF.Square), reads=[yv], writes=[y2])
                p.op("act", lambda e: e.activation(out=y2.t[:], in_=y2.t[:], func=AF.Identity, scale=0.044715, bias=1.0), reads=[y2], writes=[y2])
                p.op("dve", lambda e: e.tensor_tensor(out=y2.t[:], in0=y2.t[:], in1=yv.t[:], op=ALU.mult), reads=[y2, yv], writes=[y2])
                p.op("act", lambda e: e.activation(out=sgm.t[:], in_=y2.t[:], func=AF.Sigmoid, scale=2.0 * math.sqrt(2.0 / math.pi)),
                     reads=[y2], writes=[sgm])
                p.op("dve", lambda e: e.tensor_tensor(out=GK_.t[:, cols], in0=yv.t[:], in1=sgm.t[:], op=ALU.mult),
                     reads=[yv, sgm], writes=[GK_])

            for blk in range(NB):
                cols = slice(blk * 512, (blk + 1) * 512)
                PY_ = psb[4 + blk % 2]
                for s4 in range(4):
                    s = 4 * k + s4
                    R = slice(32 * s4, 32 * s4 + 32)
                    b2 = n % 2
                    b3 = n % 3
                    n += 1
                    PR_, PI_ = psb[2 * b2], psb[2 * b2 + 1]
                    if s4 % 2 == 0:
                        BW_, RK = Bl, R
                    else:
                        BW_, RK = BlO, slice(64 * (s4 // 2), 64 * (s4 // 2) + 64)
                    p.op("pe", lambda e: e.matmul(PR_.t[:], lhsT=BW_.t[RK, k, 0, :], rhs=HK_.t[RK, cols], start=True, stop=True),
                         reads=[BW_, HK_], writes=[PR_])
                    p.op("pe", lambda e: e.matmul(PI_.t[:], lhsT=BW_.t[RK, k, 1, :], rhs=HK_.t[RK, cols], start=True, stop=True),
                         reads=[BW_, HK_], writes=[PI_])
                    CS_, SN_, RT_ = cosT[s4], sinT[s4], rT[s4]
                    p.op("dve", lambda e: e.tensor_tensor(out=m1[b2].t[:], in0=PR_.t[:], in1=CS_.t[:], op=ALU.mult), reads=[PR_, CS_], writes=[m1[b2]])
                    p.op("dve", lambda e: e.tensor_tensor(out=m2[b2].t[:], in0=PI_.t[:], in1=SN_.t[:], op=ALU.mult), reads=[PI_, SN_], writes=[m2[b2]])
                    p.op("dve", lambda e: e.tensor_tensor(out=mre[b2].t[:], in0=m1[b2].t[:], in1=m2[b2].t[:], op=ALU.add), reads=[m1[b2], m2[b2]], writes=[mre[b2]])
                    p.op("dve", lambda e: e.tensor_tensor(out=m1[b2].t[:], in0=PI_.t[:], in1=CS_.t[:], op=ALU.mult), reads=[PI_, CS_, mre[b2]], writes=[m1[b2]])
                    p.op("dve", lambda e: e.tensor_tensor(out=m2[b2].t[:], in0=PR_.t[:], in1=SN_.t[:], op=ALU.mult), reads=[PR_, SN_, mre[b2]], writes=[m2[b2]])
                    p.op("dve", lambda e: e.tensor_tensor(out=mim[b2].t[:], in0=m1[b2].t[:], in1=m2[b2].t[:], op=ALU.subtract), reads=[m1[b2], m2[b2]], writes=[mim[b2]])
                    while len(pend) > 1:
                        for fn in pend.pop(0):
                            fn()
                    ire = 0.0 if blk == 0 else carry[s4].t[:, 0:1]
                    iim = 0.0 if blk == 0 else carry[s4].t[:, 1:2]
                    p.op("dve", lambda e: e.tensor_tensor_scan(out=wre[b2].t[:], data0=RT_.t[:], data1=mre[b2].t[:], initial=ire,
                                                               op0=ALU.mult, op1=ALU.add), reads=[RT_, mre[b2], carry[s4]], writes=[wre[b2]])
                    p.op("dve", lambda e: e.tensor_tensor_scan(out=wim[b2].t[:], data0=RT_.t[:], data1=mim[b2].t[:], initial=iim,
                                                               op0=ALU.mult, op1=ALU.add), reads=[RT_, mim[b2], carry[s4]], writes=[wim[b2]])
                    DE = "pool"
                    p.op(DE, lambda e: e.tensor_tensor(out=d1[b2].t[:], in0=wre[b2].t[:], in1=CS_.t[:], op=ALU.mult), reads=[wre[b2], CS_], writes=[d1[b2]])
                    p.op(DE, lambda e: e.tensor_tensor(out=d2[b2].t[:], in0=wim[b2].t[:], in1=SN_.t[:], op=ALU.mult), reads=[wim[b2], SN_], writes=[d2[b2]])
                    p.op(DE, lambda e: e.tensor_tensor(out=xre[b3].t[:], in0=d1[b2].t[:], in1=d2[b2].t[:], op=ALU.subtract), reads=[d1[b2], d2[b2]], writes=[xre[b3]])
                    p.op(DE, lambda e: e.tensor_tensor(out=d3[b3].t[:], in0=wre[b2].t[:], in1=SN_.t[:], op=ALU.mult), reads=[wre[b2], SN_], writes=[d3[b3]])
                    p.op(DE, lambda e: e.tensor_tensor(out=d4[b3].t[:], in0=wim[b2].t[:], in1=CS_.t[:], op=ALU.mult), reads=[wim[b2], CS_], writes=[d4[b3]])
                    fns = [lambda s=s, s4=s4, b3=b3, PY_=PY_: tail(s, s4, b3, PY_)]
                    if s4 == 3:
                        fns.append(lambda cols=cols, PY_=PY_: evac(cols, PY_))
                    pend.append(fns)
            for fns in pend:
                for fn in fns:
                    fn()
            pend = []
            p.dma("sp", c.OT[k, :, :], GK_.t[:], reads=[GK_], writes=c.OTk)
        p.barrier()


def phase_glu(c, w_ap, NFC):
    p, nc = c.p, c.nc
    wv_ = w_ap.rearrange("(f p) d -> p f d", p=128)
    with contextlib.ExitStack() as ph:
        wvl = [p.sb([128, NFC, 512], BF16, ph) for _ in range(2)]
        wgt = [p.sb([128, NFC, 512], BF16, ph) for _ in range(2)]
        oT = [p.sb([128, NFC, 512], BF16, ph) for _ in range(3)]
        sg = [p.sb([128, 512], F32, ph) for _ in range(2)]
        mo = [p.sb([128, 512], F32, ph) for _ in range(4)]
        items = [(cg, blk) for cg in range(4) for blk in range(c.NB)]

        xpb = [p.sb([128, 512], F32, ph) for _ in range(12)]

        def load(i):
            cg, blk = items[i]
            p.dma("sp", oT[i % 3].t[:], c.OT[0:NFC, :, blk * 512:(blk + 1) * 512].rearrange("c p t -> p c t"),
                  reads=[c.OTk[4 * blk + j] for j in range(4)], writes=[oT[i % 3]])
            for tt in range(4):
                t = 4 * blk + tt
                XP_ = xpb[(i % 3) * 4 + tt]
                p.dma("sp", XP_.t[:], c.X[t * 128:(t + 1) * 128, cg * 512:(cg + 1) * 512], reads=[c.Xk[t][cg]], writes=[XP_])

        def loadw(cg):
            WV_, WG_ = wvl[cg % 2], wgt[cg % 2]
            for f in range(0, NFC, 8):
                p.dma("pool", WV_.t[:, f:f + 8, :], wv_[:, f:f + 8, cg * 512:(cg + 1) * 512], writes=[WV_])
                p.dma("pool", WG_.t[:, f:f + 8, :], wv_[:, f:f + 8, D + cg * 512:D + (cg + 1) * 512], writes=[WG_])
        loadw(0)
        load(0)
        load(1)
        n = 0
        for i, (cg, blk) in enumerate(items):
            if blk == 0 and cg + 1 < 4:
                loadw(cg + 1)
            if i + 2 < len(items):
                load(i + 2)
            WV_, WG_ = wvl[cg % 2], wgt[cg % 2]
            O_ = oT[i % 3]
            for tt in range(4):
                t = 4 * blk + tt
                M_ = mo[n % 4]
                SG_ = sg[n % 2]
                PV_, PG_ = c.psb[2 * (n % 4)], c.psb[2 * (n % 4) + 1]
                n += 1
                for P_, W_ in ((PV_, WV_), (PG_, WG_)):
                    for f in range(NFC):
                        p.op("pe", lambda e: e.matmul(P_.t[:], lhsT=O_.t[:, f, tt * 128:(tt + 1) * 128], rhs=W_.t[:, f, :],
                                                      start=(f == 0), stop=(f == NFC - 1)),
                             reads=[O_, W_], writes=[P_], inc=(f == NFC - 1))
                p.op("act", lambda e: e.activation(out=SG_.t[:], in_=PG_.t[:], func=AF.Sigmoid), reads=[PG_], writes=[SG_])
                XP_ = xpb[(i % 3) * 4 + tt]
                p.op("dve", lambda e: e.tensor_tensor(out=M_.t[:], in0=PV_.t[:], in1=SG_.t[:], op=ALU.mult), reads=[PV_, SG_], writes=[M_])
                p.op("dve", lambda e: e.tensor_tensor(out=M_.t[:], in0=M_.t[:], in1=XP_.t[:], op=ALU.add), reads=[M_, XP_], writes=[M_])
                p.dma("sp", c.X[t * 128:(t + 1) * 128, cg * 512:(cg + 1) * 512], M_.t[:], reads=[M_], writes=[c.Xk[t][cg]])
        p.barrier()


def make_layers_and_inputs(I, kinds, S, batch=0):
    layers = []
    im = {"x": np.ascontiguousarray(I["x"][batch, :S]), "ident": np.eye(128, dtype=np.float32)}
    gains = []
    cnt = {"a": 0, "b": 0, "c": 0}
    for i, kind in enumerate(kinds):
        L = {"kind": kind}
        gains.append(I["norm_mix"][i])
        gains.append(I["norm_mlp"][i])
        im["w1_%d" % i] = I["mlp_w1"][i]
        im["w2_%d" % i] = I["mlp_w2"][i]
        if kind != "none":
            j = cnt[kind]
            cnt[kind] += 1
        if kind == "a":
            L["NH"] = 8
            L["lambda_init"] = 0.8 - 0.6 * math.exp(-0.3 * i)
            w = I["a_w_in"][j]
            im["a_wq_%d" % i] = np.ascontiguousarray(w[:, 0:D])
            im["a_wk_%d" % i] = np.ascontiguousarray(w[:, D:2 * D])
            im["a_wv_%d" % i] = np.ascontiguousarray(w[:, 2 * D:3 * D])
            im["a_wo_%d" % i] = I["a_w_out"][j]
            im["a_lam_%d" % i] = I["a_lambda"][j]
            im["a_subln_%d" % i] = I["a_subln"][j]
        if kind == "b":
            L["NG"] = 128
            for nm in ("a_re", "a_im", "log_dt", "b_re", "b_im", "c_re", "c_im", "d"):
                im["b_" + nm] = I["b_" + nm][j]
            im["b_wglu"] = I["b_w_glu"][j]
            im["b_iota"] = np.arange(1, 513, dtype=np.float32).reshape(1, 512)
        if kind == "c":
            L["NH"] = 8
            L["head0"] = 0
            w = I["c_w_in"][j]
            im["c_wq_%d" % i] = np.ascontiguousarray(w[:, 0:D])
            im["c_wk_%d" % i] = np.ascontiguousarray(w[:, D:2 * D])
            im["c_wv_%d" % i] = np.ascontiguousarray(w[:, 2 * D:4 * D])
            im["c_wg_%d" % i] = np.ascontiguousarray(w[:, 4 * D:6 * D])
            im["c_wo_%d" % i] = I["c_w_out"][j]
            cosT, sinT, dtab, qdec, kdecT, g512 = ret_consts(S)
            im["c_cos"], im["c_sin"], im["c_dtab"], im["c_qdec"], im["c_kdecT"] = cosT, sinT, dtab, qdec, kdecT
            L["g512"] = g512
        layers.append(L)
    gains.append(I["norm_final"])
    im["gains"] = np.ascontiguousarray(np.stack(gains).astype(np.float32))
    return layers, im

_CACHE = {}
KINDS = ["a", "b", "c", "a"]
SEQ = 4096
N_ACTIVE = 4


def kernel(**inputs):
    I = {k: np.asarray(v) for k, v in inputs.items()}
    S = SEQ
    layers, im0 = make_layers_and_inputs(I, KINDS, S, batch=0)
    key = (S, tuple(KINDS))
    if key not in _CACHE:
        nc, c = build_program(S, layers, 8192)
        _CACHE[key] = nc
    nc = _CACHE[key]
    in_maps = []
    for core in range(N_ACTIVE):
        im = dict(im0)
        im["x"] = np.ascontiguousarray(I["x"][core, :S]).astype(np.float32, copy=False)
        in_maps.append(im)
    res = run_bass_kernel_spmd(nc, in_maps, core_ids=list(range(N_ACTIVE)))
    out = np.stack([np.asarray(res.results[b]["y"]) for b in range(N_ACTIVE)], axis=0)
    return out.astype(np.float32, copy=False)
```

```python
import contextlib
import os as _os
import math
import numpy as np
import concourse.bass as bass
import concourse.mybir as mybir
from concourse.alu_op_type import AluOpType as ALU
from concourse.bass_utils import run_bass_kernel_spmd

AF = mybir.ActivationFunctionType
AX = mybir.AxisListType
F32 = mybir.dt.float32
BF16 = mybir.dt.bfloat16
I32 = mybir.dt.int32
NDS = 8

D = 2048
NDC = 16
EPS = 1e-6
TWO_PI = 2.0 * math.pi


class Tok:
    __slots__ = ("name", "w", "r")

    def __init__(self, name=""):
        self.name = name
        self.w = None
        self.r = []


class Buf:
    __slots__ = ("t", "k")

    def __init__(self, t):
        self.t = t
        self.k = Tok()


class Prog:
    def __init__(self, nc, stack):
        self.nc = nc
        self.st = stack
        self.engs = {"pe": nc.tensor, "dve": nc.vector, "act": nc.scalar,
                     "pool": nc.gpsimd, "sp": nc.sync}
        self.sem = {k: stack.enter_context(nc.semaphore("c_" + k)) for k in self.engs}
        self.cnt = {k: 0 for k in self.engs}
        self.seen = {k: {} for k in self.engs}
        self.dsem = {q: [stack.enter_context(nc.semaphore("d_%s%d" % (q, i))) for i in range(NDS)]
                     for q in ("sp", "pool", "act")}
        self.dcnt = {q: 0 for q in self.dsem}
        self.ntile = 0
        self.nins = 0

    def sb(self, shape, dtype, st=None):
        self.ntile += 1
        t = (st or self.st).enter_context(self.nc.sbuf_tensor("sb%d" % self.ntile, list(shape), dtype))
        return Buf(t)

    def ps(self, shape, dtype=F32):
        self.ntile += 1
        t = self.st.enter_context(self.nc.psum_tensor("ps%d" % self.ntile, list(shape), dtype))
        return Buf(t)

    def _wait(self, E, ev):
        sem, val = ev
        key = id(sem)
        if self.seen[E].get(key, 0) >= val:
            return
        self.engs[E].wait_ge(sem, val)
        self.seen[E][key] = val

    def _deps(self, E, reads, writes):
        best = {}

        def add(ev):
            k = id(ev[0])
            if k not in best or best[k][1] < ev[1]:
                best[k] = ev
        for t in reads:
            if t.w is not None:
                add(t.w)
        for t in writes:
            if t.w is not None:
                add(t.w)
            for ev in t.r:
                add(ev)
        for ev in best.values():
            if E == "pe" and ev[0] is self.sem["pe"]:
                continue
            self._wait(E, ev)

    def _commit(self, ev, reads, writes):
        for t in reads:
            t.r.append(ev)
            if len(t.r) > 48:
                best = {}
                for sem, val in t.r:
                    k = id(sem)
                    if k not in best or best[k][1] < val:
                        best[k] = (sem, val)
                t.r = list(best.values())
        for t in writes:
            t.w = ev
            t.r = []

    def op(self, E, fn, reads=(), writes=(), inc=True):
        reads = [b.k if isinstance(b, Buf) else b for b in reads]
        writes = [b.k if isinstance(b, Buf) else b for b in writes]
        self._deps(E, reads, writes)
        ins = fn(self.engs[E])
        self.nins += 1
        if inc:
            self.cnt[E] += 1
            ins.then_inc(self.sem[E], 1)
            ev = (self.sem[E], self.cnt[E])
        else:
            ev = (self.sem[E], self.cnt[E] + 1)
        self._commit(ev, reads, writes)
        return ev

    def dma(self, q, out, in_, reads=(), writes=(), **kw):
        reads = [b.k if isinstance(b, Buf) else b for b in reads]
        writes = [b.k if isinstance(b, Buf) else b for b in writes]
        i = self.dcnt[q]
        s = self.dsem[q][i % NDS]
        if i >= NDS:
            self._wait(q, (s, 16 * (i // NDS)))
        self._deps(q, reads, writes)
        self.engs[q].dma_start(out=out, in_=in_, **kw).then_inc(s, 16)
        self.nins += 1
        ev = (s, 16 * (i // NDS + 1))
        self.dcnt[q] += 1
        self._commit(ev, reads, writes)
        return ev

    def cc(self, kind, op, groups, in_ap, out_ap, reads=(), writes=()):
        reads = [b.k if isinstance(b, Buf) else b for b in reads]
        writes = [b.k if isinstance(b, Buf) else b for b in writes]
        q = "pool"
        i = self.dcnt[q]
        s = self.dsem[q][i % NDS]
        if i >= NDS:
            self._wait(q, (s, 16 * (i // NDS)))
        self._deps(q, reads, writes)
        self.nc.gpsimd.collective_compute(kind, op, replica_groups=groups, ins=[in_ap], outs=[out_ap]).then_inc(s, 16)
        self.nins += 1
        ev = (s, 16 * (i // NDS + 1))
        self.dcnt[q] += 1
        self._commit(ev, reads, writes)
        return ev

    def barrier(self):
        evs = []
        for k in self.engs:
            if self.cnt[k] > 0:
                evs.append((self.sem[k], self.cnt[k]))
        for q, l in self.dsem.items():
            n = self.dcnt[q]
            for j, s in enumerate(l):
                c = (n - j + NDS - 1) // NDS
                if c > 0:
                    evs.append((s, 16 * c))
        for E in self.engs:
            for ev in evs:
                if ev[0] is self.sem[E]:
                    continue
                self._wait(E, ev)


class Ctx:
    pass


def dram_in(nc, name, shape, dtype=F32):
    return nc.dram_tensor(name, list(shape), dtype, kind="ExternalInput").ap()


def build_program(S, layers, FF, dbg_x=False, dbg_mode=None):
    nc = bass.Bass("TRN2", target_bir_lowering=False)
    NT = S // 128
    NB = S // 512
    c = Ctx()
    c.nc, c.S, c.NT, c.NB, c.FF = nc, S, NT, NB, FF
    c.x_in = dram_in(nc, "x", [S, D])
    c.gains = dram_in(nc, "gains", [2 * len(layers) + 1, D])
    c.ident_in = dram_in(nc, "ident", [128, 128])
    c.y = nc.dram_tensor("y", [S, D], F32, kind="ExternalOutput").ap()
    c.X = nc.dram_tensor("Xs", [S, D], F32, kind="Internal").ap()
    c.MIX = nc.dram_tensor("MIXs", [S, D], F32, kind="Internal").ap()
    c.HT = nc.dram_tensor("HTs", [NDC, 128, S], BF16, kind="Internal").ap()
    c.Xk = [[Tok() for _ in range(4)] for _ in range(NT)]
    c.MIXk = [Tok() for _ in range(NT)]
    c.HTk = [Tok() for _ in range(NT)]
    c.yk = [Tok() for _ in range(NT)]
    c.OT = nc.dram_tensor("OTs", [32, 128, S], BF16, kind="Internal").ap()
    c.OTk = [Tok() for _ in range(NT)]
    NFG_ = FF // 512
    c.W1S = nc.dram_tensor("W1Ss", [NFG_, 128, NDC * 512], BF16, kind="Internal").ap()
    c.W2S = nc.dram_tensor("W2Ss", [4 * NFG_, 128, 4 * 512], BF16, kind="Internal").ap()
    c.W1Sk = [Tok() for _ in range(NFG_)]
    c.W2Sk = [Tok() for _ in range(4 * NFG_)]
    for li, L in enumerate(layers):
        L["w1"] = dram_in(nc, "w1_%d" % li, [D, FF])
        L["w2"] = dram_in(nc, "w2_%d" % li, [FF, D])
        if L["kind"] == "a":
            NH = L["NH"]
            L["wq"] = dram_in(nc, "a_wq_%d" % li, [D, NH * 256])
            L["wk"] = dram_in(nc, "a_wk_%d" % li, [D, NH * 256])
            L["wv"] = dram_in(nc, "a_wv_%d" % li, [D, NH * 256])
            L["wo"] = dram_in(nc, "a_wo_%d" % li, [NH * 256, D])
            L["lam"] = dram_in(nc, "a_lam_%d" % li, [4, 128])
            L["subln"] = dram_in(nc, "a_subln_%d" % li, [256])
        if L["kind"] == "b":
            NG = L["NG"]
            L["a_re"] = dram_in(nc, "b_a_re", [NG, 64])
            L["a_im"] = dram_in(nc, "b_a_im", [NG, 64])
            L["log_dt"] = dram_in(nc, "b_log_dt", [NG])
            L["b_re"] = dram_in(nc, "b_b_re", [NG, 64, 16])
            L["b_im"] = dram_in(nc, "b_b_im", [NG, 64, 16])
            L["c_re"] = dram_in(nc, "b_c_re", [NG, 16, 64])
            L["c_im"] = dram_in(nc, "b_c_im", [NG, 16, 64])
            L["d"] = dram_in(nc, "b_d", [NG * 16])
            L["wglu"] = dram_in(nc, "b_wglu", [NG * 16, 2 * D])
            L["iota"] = dram_in(nc, "b_iota", [1, 512])
            c.BBS = nc.dram_tensor("BBSs", [2, NG, 16, 64], BF16, kind="Internal").ap()
        if L["kind"] == "c":
            NH = L["NH"]
            L["wq"] = dram_in(nc, "c_wq_%d" % li, [D, NH * 256])
            L["wk"] = dram_in(nc, "c_wk_%d" % li, [D, NH * 256])
            L["wv"] = dram_in(nc, "c_wv_%d" % li, [D, NH * 512])
            L["wg"] = dram_in(nc, "c_wg_%d" % li, [D, NH * 512])
            L["wo"] = dram_in(nc, "c_wo_%d" % li, [NH * 512, D])
            L["cos"] = dram_in(nc, "c_cos", [128, S])
            L["sin"] = dram_in(nc, "c_sin", [128, S])
            L["dtab"] = dram_in(nc, "c_dtab", [8, 4, 128, 512])
            L["qdec"] = dram_in(nc, "c_qdec", [8, 512])
            L["kdecT"] = dram_in(nc, "c_kdecT", [8, 128, 4])
    with contextlib.ExitStack() as st:
        p = Prog(nc, st)
        c.p = p
        c.psb = [p.ps([128, 512], F32) for _ in range(8)]
        c.ident = p.sb([128, 128], BF16)
        with contextlib.ExitStack() as ph:
            idf = p.sb([128, 128], F32, ph)
            p.dma("sp", idf.t[:], c.ident_in, writes=[idf])
            p.op("dve", lambda e: e.tensor_copy(out=c.ident.t[:], in_=idf.t[:]), reads=[idf], writes=[c.ident])
            p.barrier()
        first = True
        if dbg_mode == "mlp_nomix":
            phase_norm(c, 0, add_mix=False, from_input=True)
            phase_mlp(c, layers[0])
            phase_norm(c, 1, add_mix=False, from_input=False, final=True)
            layers = []
        if dbg_mode == "norm_only":
            phase_norm(c, 0, add_mix=False, from_input=True)
            phase_norm(c, 1, add_mix=False, from_input=False, final=True)
            layers = []
        for li, L in enumerate(layers):
            if L["kind"] != "none":
                phase_norm(c, 2 * li, add_mix=False, from_input=first)
                first = False
                if L["kind"] == "a":
                    phase_attn(c, L)
                    phase_outproj(c, L["wo"], L["NH"] * 2)
                elif L["kind"] == "b":
                    phase_s5(c, L)
                    phase_glu(c, L["wglu"], L["NG"] // 8)
                elif L["kind"] == "c":
                    phase_ret(c, L)
                    phase_outproj(c, L["wo"], L["NH"] * 4)
                else:
                    raise NotImplementedError
            phase_norm(c, 2 * li + 1, add_mix=False, from_input=first)
            first = False
            phase_mlp(c, L)
        if dbg_mode is None:
            phase_norm(c, 2 * len(layers), add_mix=False, from_input=False, final=True)
        for t in c.yk:
            if t.w is not None:
                p._wait("sp", t.w)
        p.barrier()
    c.nins = p.nins
    return nc, c


def phase_norm(c, gi, add_mix, from_input, final=False):
    p, nc = c.p, c.nc
    with contextlib.ExitStack() as ph:
        gB = p.sb([128, D], F32, ph)
        p.dma("sp", gB.t[:], c.gains[gi:gi + 1, :].broadcast_to([128, D]), writes=[gB])
        NBUF = 3
        xt = [p.sb([128, D], F32, ph) for _ in range(NBUF)]
        mt = [p.sb([128, D], F32, ph) for _ in range(NBUF)]
        junk = [p.sb([128, D], BF16, ph) for _ in range(NBUF)]
        st_ = [p.sb([128, 4], F32, ph) for _ in range(NBUF)]
        if final:
            ob = [p.sb([128, D], F32, ph) for _ in range(NBUF)]
        else:
            hb = [p.sb([128, D], BF16, ph) for _ in range(NBUF)]
            hT = [p.sb([128, NDC, 128], BF16, ph) for _ in range(NBUF)]
        src = c.x_in if from_input else c.X

        def loads(t):
            b = t % NBUF
            rows = slice(t * 128, (t + 1) * 128)
            p.dma("sp", xt[b].t[:], src[rows, :], reads=[] if from_input else c.Xk[t], writes=[xt[b]])
            if add_mix:
                p.dma("sp", mt[b].t[:], c.MIX[rows, :], reads=[c.MIXk[t]], writes=[mt[b]])
        loads(0)
        if c.NT > 1:
            loads(1)
        for t in range(c.NT):
            b = t % NBUF
            X_, M_, J_, S_ = xt[b], mt[b], junk[b], st_[b]
            rows = slice(t * 128, (t + 1) * 128)
            if t + 2 < c.NT:
                loads(t + 2)
            if add_mix:
                p.op("dve", lambda e: e.tensor_tensor(out=X_.t[:], in0=X_.t[:], in1=M_.t[:], op=ALU.add),
                     reads=[X_, M_], writes=[X_])
            if (add_mix or from_input) and not final:
                p.dma("act", c.X[rows, :], X_.t[:], reads=[X_], writes=c.Xk[t])
            p.op("act", lambda e: e.activation(out=J_.t[:], in_=X_.t[:], func=AF.Square, accum_out=S_.t[:, 0:1]),
                 reads=[X_], writes=[J_, S_])
            p.op("dve", lambda e: e.tensor_scalar(out=S_.t[:, 1:2], in0=S_.t[:, 0:1], scalar1=1.0 / D, scalar2=EPS,
                                                  op0=ALU.mult, op1=ALU.add), reads=[S_], writes=[S_])
            p.op("act", lambda e: e.activation(out=S_.t[:, 2:3], in_=S_.t[:, 1:2], func=AF.Sqrt), reads=[S_], writes=[S_])
            p.op("dve", lambda e: e.reciprocal(out=S_.t[:, 3:4], in_=S_.t[:, 2:3]), reads=[S_], writes=[S_])
            if final:
                O_ = ob[b]
                p.op("dve", lambda e: e.scalar_tensor_tensor(out=O_.t[:], in0=X_.t[:], scalar=S_.t[:, 3:4], in1=gB.t[:],
                                                             op0=ALU.mult, op1=ALU.mult), reads=[X_, S_, gB], writes=[O_])
                p.dma("act", c.y[rows, :], O_.t[:], reads=[O_], writes=[c.yk[t]])
                continue
            H_, T_ = hb[b], hT[b]
            p.op("dve", lambda e: e.scalar_tensor_tensor(out=H_.t[:], in0=X_.t[:], scalar=S_.t[:, 3:4], in1=gB.t[:],
                                                         op0=ALU.mult, op1=ALU.mult), reads=[X_, S_, gB], writes=[H_])
            for half in range(2):
                P_ = c.psb[(2 * t + half) % 2]
                pv = P_.t[:].bitcast(BF16)
                for j in range(8):
                    dc = half * 8 + j
                    p.op("pe", lambda e: e.transpose(out=pv[:, j * 128:(j + 1) * 128],
                                                     in_=H_.t[:, dc * 128:(dc + 1) * 128], identity=c.ident.t[:]),
                         reads=[H_, c.ident], writes=[P_], inc=(j == 7))
                p.op("act", lambda e: e.activation(out=T_.t[:, half * 8:(half + 1) * 8, :],
                                                   in_=pv.rearrange("p (j q) -> p j q", j=8), func=AF.Copy),
                     reads=[P_], writes=[T_])
            p.dma("sp", c.HT[:, :, rows].rearrange("c p t -> p c t"), T_.t[:], reads=[T_], writes=[c.HTk[t]])
        p.barrier()


def phase_mlp(c, L):
    p, nc = c.p, c.nc
    FF = c.FF
    NF = FF // 128
    NFG = FF // 512
    w1v = L["w1"].rearrange("(k p) f -> p k f", p=128)
    w2v = L["w2"].rearrange("(f p) d -> p f d", p=128)
    with contextlib.ExitStack() as ph:
        hT = [p.sb([128, NDC, 512], BF16, ph) for _ in range(2)]
        aT = p.sb([128, NF, 512], BF16, ph)
        w1g = [p.sb([128, NDC, 512], BF16, ph) for _ in range(2)]
        w2g = [p.sb([128, 4, 512], BF16, ph) for _ in range(3)]
        rt = [p.sb([128, 512], F32, ph) for _ in range(2)]
        mo = [p.sb([128, 512], F32, ph) for _ in range(4)]
        xpb = [p.sb([128, 512], F32, ph) for _ in range(8)]
        nxp = 0
        aTk = [Tok() for _ in range(NFG)]
        nw1 = nw2 = nr = nm = 0
        for blk in range(c.NB):
            H_ = hT[blk % 2]
            cols = slice(blk * 512, (blk + 1) * 512)
            p.dma("sp", H_.t[:], c.HT[:, :, cols].rearrange("c p t -> p c t"),
                  reads=[c.HTk[4 * blk + i] for i in range(4)], writes=[H_])
            for fg in range(NFG):
                W_ = w1g[nw1 % 2]
                nw1 += 1
                if blk == 0:
                    p.dma("pool", W_.t[:], w1v[:, :, fg * 512:(fg + 1) * 512], writes=[W_])
                    p.dma("sp", c.W1S[fg], W_.t[:].rearrange("p k f -> p (k f)"), reads=[W_], writes=[c.W1Sk[fg]])
                else:
                    p.dma("pool", W_.t[:].rearrange("p k f -> p (k f)"), c.W1S[fg], reads=[c.W1Sk[fg]], writes=[W_])
                for fi in range(4):
                    f = fg * 4 + fi
                    P_ = c.psb[4 + (f % 2)]
                    for k in range(NDC):
                        p.op("pe", lambda e: e.matmul(P_.t[:], lhsT=W_.t[:, k, fi * 128:(fi + 1) * 128], rhs=H_.t[:, k, :],
                                                      start=(k == 0), stop=(k == NDC - 1)),
                             reads=[W_, H_], writes=[P_], inc=(k == NDC - 1))
                    R_ = rt[nr % 2]
                    nr += 1
                    p.op("act", lambda e: e.activation(out=R_.t[:], in_=P_.t[:], func=AF.Relu), reads=[P_], writes=[R_])
                    p.op("dve", lambda e: e.tensor_tensor(out=aT.t[:, f, :], in0=R_.t[:], in1=R_.t[:], op=ALU.mult),
                         reads=[R_], writes=[aTk[fg]])
            for cg in range(4):
                XP = []
                for tt in range(4):
                    t = 4 * blk + tt
                    XP_ = xpb[nxp % 8]
                    nxp += 1
                    p.dma("sp", XP_.t[:], c.X[t * 128:(t + 1) * 128, cg * 512:(cg + 1) * 512], reads=[c.Xk[t][cg]], writes=[XP_])
                    XP.append(XP_)
                for fg in range(NFG):
                    W_ = w2g[nw2 % 3]
                    nw2 += 1
                    if blk == 0:
                        p.dma("pool", W_.t[:], w2v[:, fg * 4:(fg + 1) * 4, cg * 512:(cg + 1) * 512], writes=[W_])
                        p.dma("sp", c.W2S[cg * NFG + fg], W_.t[:].rearrange("p k f -> p (k f)"), reads=[W_],
                              writes=[c.W2Sk[cg * NFG + fg]])
                    else:
                        p.dma("pool", W_.t[:].rearrange("p k f -> p (k f)"), c.W2S[cg * NFG + fg],
                              reads=[c.W2Sk[cg * NFG + fg]], writes=[W_])
                    for fi in range(4):
                        f = fg * 4 + fi
                        for tt in range(4):
                            P_ = c.psb[tt]
                            p.op("pe", lambda e: e.matmul(P_.t[:], lhsT=aT.t[:, f, tt * 128:(tt + 1) * 128], rhs=W_.t[:, fi, :],
                                                          start=(f == 0), stop=(f == NF - 1)),
                                 reads=[W_, aTk[fg]], writes=[P_], inc=(f == NF - 1 or (fi == 3 and tt == 3)))
                for tt in range(4):
                    M_ = mo[nm % 4]
                    nm += 1
                    P_ = c.psb[tt]
                    t = 4 * blk + tt
                    p.op("dve", lambda e: e.tensor_tensor(out=M_.t[:], in0=P_.t[:], in1=XP[tt].t[:], op=ALU.add),
                         reads=[P_, XP[tt]], writes=[M_])
                    p.dma("sp", c.X[t * 128:(t + 1) * 128, cg * 512:(cg + 1) * 512], M_.t[:], reads=[M_],
                          writes=[c.Xk[t][cg]])
        p.barrier()


def load_cols(p, ph, vec_ap, n, scale=None):
    t = p.sb([128, n], F32, ph)
    p.dma("sp", t.t[:], vec_ap.rearrange("(c p) -> p c", p=128), writes=[t], allow_slow_non_contiguous=True)
    if scale is not None:
        p.op("dve", lambda e: e.tensor_scalar(out=t.t[:], in0=t.t[:], scalar1=float(scale), scalar2=None, op0=ALU.mult),
             reads=[t], writes=[t])
    return t


def make_ones(c, ph):
    p = c.p
    o32 = p.sb([128, 128], F32, ph)
    ob = p.sb([128, 128], BF16, ph)
    p.op("dve", lambda e: e.memset(o32.t[:], 1.0), writes=[o32])
    p.op("dve", lambda e: e.tensor_copy(out=ob.t[:], in_=o32.t[:]), reads=[o32], writes=[ob])
    return ob


def phase_attn(c, L):
    p, nc = c.p, c.nc
    S, NT, NB = c.S, c.NT, c.NB
    NH = L["NH"]
    lam_init = L["lambda_init"]
    wq = L["wq"].rearrange("(k p) f -> p k f", p=128)
    wk = L["wk"].rearrange("(k p) f -> p k f", p=128)
    wv = L["wv"].rearrange("(k p) f -> p k f", p=128)
    with contextlib.ExitStack() as ph:
        ones = make_ones(c, ph)
        lp = p.sb([128, 4, 128], F32, ph)
        p.dma("sp", lp.t[:].rearrange("p a b -> p (a b)"),
              L["lam"].rearrange("a b -> (a b)").rearrange("(o n) -> o n", o=1).broadcast_to([128, 512]), writes=[lp])
        lt = p.sb([128, 8], F32, ph)
        pr = p.sb([128, 2, 128], F32, ph)
        p.op("dve", lambda e: e.tensor_tensor(out=pr.t[:, 0, :], in0=lp.t[:, 0, :], in1=lp.t[:, 1, :], op=ALU.mult),
             reads=[lp], writes=[pr])
        p.op("dve", lambda e: e.tensor_tensor(out=pr.t[:, 1, :], in0=lp.t[:, 2, :], in1=lp.t[:, 3, :], op=ALU.mult),
             reads=[lp, pr], writes=[pr])
        p.op("dve", lambda e: e.tensor_reduce(out=lt.t[:, 0:2], in_=pr.t[:], axis=AX.X, op=ALU.add), reads=[pr], writes=[lt])
        p.op("act", lambda e: e.activation(out=lt.t[:, 2:4], in_=lt.t[:, 0:2], func=AF.Exp), reads=[lt], writes=[lt])
        p.op("dve", lambda e: e.tensor_tensor(out=lt.t[:, 4:5], in0=lt.t[:, 3:4], in1=lt.t[:, 2:3], op=ALU.subtract),
             reads=[lt], writes=[lt])
        p.op("dve", lambda e: e.tensor_scalar(out=lt.t[:, 5:6], in0=lt.t[:, 4:5], scalar1=-float(lam_init), scalar2=None,
                                              op0=ALU.add), reads=[lt], writes=[lt])
        nlam = lt.t[:, 5:6]
        sg = load_cols(p, ph, L["subln"], 2, scale=1.0 - lam_init)
        QT = p.sb([128, 2, S], BF16, ph)
        KT = p.sb([128, 2, S], BF16, ph)
        V = p.sb([128, NT, 256], BF16, ph)
        QTk = [Tok() for _ in range(NB)]
        KTk = [Tok() for _ in range(NB)]
        Vk = [Tok() for _ in range(NB)]
        wqb = [p.sb([128, NDC, 256], BF16, ph) for _ in range(2)]
        wkb = [p.sb([128, NDC, 256], BF16, ph) for _ in range(2)]
        wvb = [p.sb([128, NDC, 256], BF16, ph) for _ in range(2)]
        hT = [p.sb([128, NDC, 512], BF16, ph) for _ in range(2)]
        PT = [p.sb([128, 512], BF16, ph) for _ in range(3)]
        rden = [p.sb([128, 512], F32, ph) for _ in range(2)]
        om = [p.sb([128, 2, 512], F32, ph) for _ in range(2)]
        o_ = p.sb([128, 2, 512], F32, ph)
        sq = p.sb([128, 2, 512], BF16, ph)
        rs = p.sb([128, 512], F32, ph)
        obf = [p.sb([128, 2, 512], BF16, ph) for _ in range(2)]
        psb = c.psb
        nh = npt = nob = 0
        pend_fin = []

        def load_w(h):
            b = h % 2
            p.dma("pool", wqb[b].t[:], wq[:, :, h * 256:(h + 1) * 256], writes=[wqb[b]])
            p.dma("pool", wkb[b].t[:], wk[:, :, h * 256:(h + 1) * 256], writes=[wkb[b]])
            p.dma("pool", wvb[b].t[:], wv[:, :, h * 256:(h + 1) * 256], writes=[wvb[b]])
        load_w(0)
        for h in range(NH):
            b = h % 2
            if h + 1 < NH:
                load_w(h + 1)
            nev = 0
            for blk in range(NB):
                H_ = hT[nh % 2]
                nh += 1
                cols = slice(blk * 512, (blk + 1) * 512)
                p.dma("sp", H_.t[:], c.HT[:, :, cols].rearrange("c p t -> p c t"),
                      reads=[c.HTk[4 * blk + i] for i in range(4)], writes=[H_])
                for which, W_, dst, dk, scl in ((0, wqb[b], QT, QTk, 128.0 ** -0.5), (1, wkb[b], KT, KTk, 1.0)):
                    for m in range(2):
                        P_ = psb[nev % 2]
                        nev += 1
                        for k in range(NDC):
                            p.op("pe", lambda e: e.matmul(P_.t[:], lhsT=W_.t[:, k, m * 128:(m + 1) * 128], rhs=H_.t[:, k, :],
                                                          start=(k == 0), stop=(k == NDC - 1)),
                                 reads=[W_, H_], writes=[P_], inc=(k == NDC - 1))
                        p.op("act", lambda e: e.activation(out=dst.t[:, m, cols], in_=P_.t[:], func=AF.Copy, scale=scl),
                             reads=[P_], writes=[dk[blk]])
                for tt in range(4):
                    P_ = psb[nev % 2]
                    nev += 1
                    W_ = wvb[b]
                    for k in range(NDC):
                        p.op("pe", lambda e: e.matmul(P_.t[:, 0:256], lhsT=H_.t[:, k, tt * 128:(tt + 1) * 128], rhs=W_.t[:, k, :],
                                                      start=(k == 0), stop=(k == NDC - 1)),
                             reads=[W_, H_], writes=[P_], inc=(k == NDC - 1))
                    p.op("dve", lambda e: e.tensor_copy(out=V.t[:, 4 * blk + tt, :], in_=P_.t[:, 0:256]),
                         reads=[P_], writes=[Vk[blk]])
            for qb in range(NB):
                nkt = 4 * qb + 4
                for m in range(2):
                    psO = [psb[2 + 3 * m], psb[3 + 3 * m]]
                    psD = psb[4 + 3 * m]
                    def emit_s(kt):
                        nonlocal npt
                        j = kt - 4 * qb
                        c0 = 128 * j if j >= 0 else 0
                        PS_ = psb[kt % 2]
                        p.op("pe", lambda e: e.matmul(PS_.t[:, c0:512], lhsT=KT.t[:, m, kt * 128:(kt + 1) * 128],
                                                      rhs=QT.t[:, m, qb * 512 + c0:(qb + 1) * 512], start=True, stop=True),
                             reads=[KTk[kt // 4], QTk[qb]], writes=[PS_])
                        T_ = PT[npt % 3]
                        npt += 1
                        p.op("act", lambda e: e.activation(out=T_.t[:, c0:512], in_=PS_.t[:, c0:512], func=AF.Exp),
                             reads=[PS_], writes=[T_])
                        if j >= 0:
                            p.op("dve", lambda e: e.memset(T_.t[64:128, c0:c0 + 64], 0.0), writes=[T_])
                        return kt, T_, c0

                    def emit_pv(kt, T_, c0):
                        last = (kt == nkt - 1)
                        for cc in range(2):
                            p.op("pe", lambda e: e.matmul(psO[cc].t[:, c0:512], lhsT=V.t[:, kt, cc * 128:(cc + 1) * 128],
                                                          rhs=T_.t[:, c0:512], start=(kt == 0), stop=last),
                                 reads=[Vk[kt // 4], T_], writes=[psO[cc]], inc=last)
                        p.op("pe", lambda e: e.matmul(psD.t[:, c0:512], lhsT=ones.t[:], rhs=T_.t[:, c0:512],
                                                      start=(kt == 0), stop=last),
                             reads=[ones, T_], writes=[psD], inc=True)
                    prev = None
                    for kt in range(nkt):
                        cur = emit_s(kt)
                        if prev is not None:
                            emit_pv(*prev)
                        prev = cur
                    emit_pv(*prev)
                    if m == 0:
                        for fn in pend_fin:
                            fn()
                        pend_fin = []
                    R_ = rden[m]
                    p.op("dve", lambda e: e.reciprocal(out=R_.t[:], in_=psD.t[:]), reads=[psD], writes=[R_])
                    for cc in range(2):
                        p.op("dve", lambda e: e.tensor_tensor(out=om[m].t[:, cc, :], in0=psO[cc].t[:], in1=R_.t[:], op=ALU.mult),
                             reads=[psO[cc], R_], writes=[om[m]])
                def finish_qb(h=h, qb=qb):
                    nonlocal nob
                    p.op("dve", lambda e: e.scalar_tensor_tensor(out=o_.t[:], in0=om[1].t[:], scalar=nlam, in1=om[0].t[:],
                                                                 op0=ALU.mult, op1=ALU.add), reads=[om[0], om[1], lt], writes=[o_])
                    p.op("act", lambda e: e.activation(out=sq.t[:], in_=o_.t[:], func=AF.Square), reads=[o_], writes=[sq])
                    PN_ = psb[0]
                    for cc in range(2):
                        p.op("pe", lambda e: e.matmul(PN_.t[:], lhsT=ones.t[:], rhs=sq.t[:, cc, :], start=(cc == 0), stop=(cc == 1)),
                             reads=[ones, sq], writes=[PN_], inc=(cc == 1))
                    p.op("dve", lambda e: e.tensor_scalar(out=rs.t[:], in0=PN_.t[:], scalar1=1.0 / 256.0, scalar2=EPS,
                                                          op0=ALU.mult, op1=ALU.add), reads=[PN_], writes=[rs])
                    p.op("act", lambda e: e.activation(out=rs.t[:], in_=rs.t[:], func=AF.Sqrt), reads=[rs], writes=[rs])
                    p.op("dve", lambda e: e.reciprocal(out=rs.t[:], in_=rs.t[:]), reads=[rs], writes=[rs])
                    OB_ = obf[nob % 2]
                    nob += 1
                    for cc in range(2):
                        p.op("dve", lambda e: e.scalar_tensor_tensor(out=OB_.t[:, cc, :], in0=o_.t[:, cc, :], scalar=sg.t[:, cc:cc + 1],
                                                                     in1=rs.t[:], op0=ALU.mult, op1=ALU.mult),
                             reads=[o_, sg, rs], writes=[OB_])
                    p.dma("sp", c.OT[2 * h:2 * h + 2, :, qb * 512:(qb + 1) * 512].rearrange("c p t -> p c t"), OB_.t[:],
                          reads=[OB_], writes=[c.OTk[4 * qb + i] for i in range(4)])
                pend_fin.append(finish_qb)
            for fn in pend_fin:
                fn()
            pend_fin = []
        p.barrier()


def phase_outproj(c, wo_ap, NFC):
    p, nc = c.p, c.nc
    wov = wo_ap.rearrange("(f p) d -> p f d", p=128)
    with contextlib.ExitStack() as ph:
        wo = [p.sb([128, NFC, 512], BF16, ph) for _ in range(2)]
        oT = [p.sb([128, NFC, 512], BF16, ph) for _ in range(3)]
        mo = [p.sb([128, 512], F32, ph) for _ in range(4)]
        items = [(cg, blk) for cg in range(4) for blk in range(c.NB)]

        xpb = [p.sb([128, 512], F32, ph) for _ in range(12)]

        def load(i):
            cg, blk = items[i]
            p.dma("sp", oT[i % 3].t[:], c.OT[0:NFC, :, blk * 512:(blk + 1) * 512].rearrange("c p t -> p c t"),
                  reads=[c.OTk[4 * blk + j] for j in range(4)], writes=[oT[i % 3]])
            for tt in range(4):
                t = 4 * blk + tt
                XP_ = xpb[(i % 3) * 4 + tt]
                p.dma("sp", XP_.t[:], c.X[t * 128:(t + 1) * 128, cg * 512:(cg + 1) * 512], reads=[c.Xk[t][cg]], writes=[XP_])

        def loadw(cg):
            W_ = wo[cg % 2]
            for f in range(0, NFC, 8):
                p.dma("pool", W_.t[:, f:f + 8, :], wov[:, f:f + 8, cg * 512:(cg + 1) * 512], writes=[W_])
        loadw(0)
        load(0)
        load(1)
        n = 0
        for i, (cg, blk) in enumerate(items):
            if blk == 0 and cg + 1 < 4:
                loadw(cg + 1)
            if i + 2 < len(items):
                load(i + 2)
            W_ = wo[cg % 2]
            O_ = oT[i % 3]
            for tt in range(4):
                t = 4 * blk + tt
                M_ = mo[n % 4]
                P_ = c.psb[n % 4]
                n += 1
                for f in range(NFC):
                    p.op("pe", lambda e: e.matmul(P_.t[:], lhsT=O_.t[:, f, tt * 128:(tt + 1) * 128], rhs=W_.t[:, f, :],
                                                  start=(f == 0), stop=(f == NFC - 1)),
                         reads=[O_, W_], writes=[P_], inc=(f == NFC - 1))
                XP_ = xpb[(i % 3) * 4 + tt]
                p.op("dve", lambda e: e.tensor_tensor(out=M_.t[:], in0=P_.t[:], in1=XP_.t[:], op=ALU.add), reads=[P_, XP_], writes=[M_])
                p.dma("sp", c.X[t * 128:(t + 1) * 128, cg * 512:(cg + 1) * 512], M_.t[:], reads=[M_], writes=[c.Xk[t][cg]])
        p.barrier()


RET_SC = 512


def ret_consts(S, NH_all=8):
    half = 128
    inv = (1.0 / (np.float32(10000.0) ** np.linspace(0.0, 1.0, half, dtype=np.float32))).astype(np.float32)
    pos = np.arange(S, dtype=np.float32)
    ang = (pos[None, :] * inv[:, None]).astype(np.float32)
    cosT = np.cos(ang).astype(np.float32)
    sinT = np.sin(ang).astype(np.float32)
    lg = np.log(1.0 - np.exp2(-5.0 - np.arange(NH_all, dtype=np.float64)))
    scale = 256.0 ** -0.5
    ql = np.arange(RET_SC)
    dtab = np.zeros((NH_all, 4, 128, RET_SC), np.float32)
    for kt in range(4):
        km = 128 * kt + np.arange(128)
        dist = np.abs(ql[None, :] - km[:, None])
        mask = (km[:, None] // 64) <= (ql[None, :] // 64)
        for h in range(NH_all):
            dtab[h, kt] = (scale * np.exp(lg[h] * dist) * mask).astype(np.float32)
    qdec = (scale * np.exp(lg[:, None] * (ql[None, :] + 1.0))).astype(np.float32)
    ml = np.arange(RET_SC)
    kdec = np.exp(lg[:, None] * (RET_SC - 1.0 - ml[None, :])).astype(np.float32)
    kdecT = np.ascontiguousarray(kdec.reshape(NH_all, 4, 128).transpose(0, 2, 1))
    g512 = np.exp(lg * RET_SC)
    return cosT, sinT, dtab, qdec, kdecT, [float(v) for v in g512]


def phase_ret(c, L):
    p, nc = c.p, c.nc
    S, NT, NB = c.S, c.NT, c.NB
    NH = L["NH"]
    wq = L["wq"].rearrange("(k p) f -> p k f", p=128)
    wk = L["wk"].rearrange("(k p) f -> p k f", p=128)
    wv = L["wv"].rearrange("(k p) f -> p k f", p=128)
    wg = L["wg"].rearrange("(k p) f -> p k f", p=128)
    psb = c.psb
    with contextlib.ExitStack() as ph:
        ones = make_ones(c, ph)
        cosT = p.sb([128, S], F32, ph)
        sinT = p.sb([128, S], F32, ph)
        p.dma("sp", cosT.t[:], L["cos"], writes=[cosT])
        p.dma("sp", sinT.t[:], L["sin"], writes=[sinT])
        wqb = p.sb([128, NDC, 256], BF16, ph)
        wkb = p.sb([128, NDC, 256], BF16, ph)
        wvb = p.sb([128, NDC, 512], BF16, ph)
        wgb = p.sb([128, NDC, 512], BF16, ph)
        dtab = p.sb([128, 4, 512], F32, ph)
        qdec = p.sb([128, 512], F32, ph)
        kdec = p.sb([128, 4], F32, ph)
        hT = [p.sb([128, NDC, 512], BF16, ph) for _ in range(2)]
        ra = p.sb([128, 2, 512], F32, ph)
        rb = p.sb([128, 2, 512], F32, ph)
        rq = p.sb([128, 2, 512], F32, ph)
        QT = p.sb([128, 2, 512], BF16, ph)
        QD = p.sb([128, 2, 512], BF16, ph)
        KT = p.sb([128, 2, 512], BF16, ph)
        Kd = p.sb([128, 4, 256], BF16, ph)
        V = p.sb([128, 4, 512], BF16, ph)
        G = p.sb([128, 4, 512], F32, ph)
        R32 = p.sb([128, 2, 512], F32, ph)
        Rb = p.sb([128, 2, 512], BF16, ph)
        PT = [p.sb([128, 512], BF16, ph) for _ in range(2)]
        o_ = p.sb([128, 4, 512], F32, ph)
        sq = p.sb([128, 4, 512], BF16, ph)
        rs = p.sb([128, 512], F32, ph)
        og = [p.sb([128, 4, 512], BF16, ph) for _ in range(2)]
        nh = nog = 0
        for h in range(NH):
            hg = L["head0"] + h
            p.dma("pool", wqb.t[:], wq[:, :, h * 256:(h + 1) * 256], writes=[wqb])
            p.dma("pool", wkb.t[:], wk[:, :, h * 256:(h + 1) * 256], writes=[wkb])
            p.dma("pool", wvb.t[:], wv[:, :, h * 512:(h + 1) * 512], writes=[wvb])
            p.dma("pool", wgb.t[:], wg[:, :, h * 512:(h + 1) * 512], writes=[wgb])
            p.dma("sp", dtab.t[:], L["dtab"][hg].rearrange("k p q -> p k q"), writes=[dtab])
            p.dma("sp", qdec.t[:], L["qdec"][hg:hg + 1, :].broadcast_to([128, 512]), writes=[qdec])
            p.dma("sp", kdec.t[:], L["kdecT"][hg], writes=[kdec])
            g512 = L["g512"][hg]
            for sc in range(NB):
                H_ = hT[nh % 2]
                nh += 1
                cols = slice(sc * 512, (sc + 1) * 512)
                p.dma("sp", H_.t[:], c.HT[:, :, cols].rearrange("c p t -> p c t"),
                      reads=[c.HTk[4 * sc + i] for i in range(4)], writes=[H_])
                for which, W_ in ((0, wqb), (1, wkb)):
                    for m in range(2):
                        P_ = psb[m]
                        for k in range(NDC):
                            p.op("pe", lambda e: e.matmul(P_.t[:], lhsT=W_.t[:, k, m * 128:(m + 1) * 128], rhs=H_.t[:, k, :],
                                                          start=(k == 0), stop=(k == NDC - 1)),
                                 reads=[W_, H_], writes=[P_], inc=(k == NDC - 1))
                    for m in range(2):
                        p.op("dve", lambda e: e.tensor_tensor(out=ra.t[:, m, :], in0=psb[m].t[:], in1=cosT.t[:, cols], op=ALU.mult),
                             reads=[psb[m], cosT], writes=[ra])
                        p.op("dve", lambda e: e.tensor_tensor(out=rb.t[:, 1 - m, :], in0=psb[m].t[:], in1=sinT.t[:, cols], op=ALU.mult),
                             reads=[psb[m], sinT], writes=[rb])
                    if which == 0:
                        p.op("dve", lambda e: e.tensor_tensor(out=rq.t[:, 0, :], in0=ra.t[:, 0, :], in1=rb.t[:, 0, :], op=ALU.subtract),
                             reads=[ra, rb], writes=[rq])
                        p.op("dve", lambda e: e.tensor_tensor(out=rq.t[:, 1, :], in0=ra.t[:, 1, :], in1=rb.t[:, 1, :], op=ALU.add),
                             reads=[ra, rb], writes=[rq])
                        p.op("act", lambda e: e.activation(out=QT.t[:], in_=rq.t[:], func=AF.Copy), reads=[rq], writes=[QT])
                        for m in range(2):
                            p.op("dve", lambda e: e.tensor_tensor(out=QD.t[:, m, :], in0=rq.t[:, m, :], in1=qdec.t[:], op=ALU.mult),
                                 reads=[rq, qdec], writes=[QD])
                    else:
                        p.op("dve", lambda e: e.tensor_tensor(out=KT.t[:, 0, :], in0=ra.t[:, 0, :], in1=rb.t[:, 0, :], op=ALU.subtract),
                             reads=[ra, rb], writes=[KT])
                        p.op("dve", lambda e: e.tensor_tensor(out=KT.t[:, 1, :], in0=ra.t[:, 1, :], in1=rb.t[:, 1, :], op=ALU.add),
                             reads=[ra, rb], writes=[KT])
                for tt in range(4):
                    P_ = psb[tt % 2]
                    for k in range(NDC):
                        p.op("pe", lambda e: e.matmul(P_.t[:], lhsT=H_.t[:, k, tt * 128:(tt + 1) * 128], rhs=wvb.t[:, k, :],
                                                      start=(k == 0), stop=(k == NDC - 1)),
                             reads=[wvb, H_], writes=[P_], inc=(k == NDC - 1))
                    p.op("act", lambda e: e.activation(out=V.t[:, tt, :], in_=P_.t[:], func=AF.Copy), reads=[P_], writes=[V])
                for gc in range(4):
                    P_ = psb[gc % 2]
                    for k in range(NDC):
                        p.op("pe", lambda e: e.matmul(P_.t[:], lhsT=wgb.t[:, k, gc * 128:(gc + 1) * 128], rhs=H_.t[:, k, :],
                                                      start=(k == 0), stop=(k == NDC - 1)),
                             reads=[wgb, H_], writes=[P_], inc=(k == NDC - 1))
                    p.op("act", lambda e: e.activation(out=G.t[:, gc, :], in_=P_.t[:], func=AF.Silu), reads=[P_], writes=[G])
                PK_ = psb[0]
                pkv = PK_.t[:].bitcast(BF16)
                for tt in range(4):
                    for cq in range(2):
                        i8 = tt * 2 + cq
                        p.op("pe", lambda e: e.transpose(out=pkv[:, i8 * 128:(i8 + 1) * 128], in_=KT.t[:, cq, tt * 128:(tt + 1) * 128],
                                                         identity=c.ident.t[:]),
                             reads=[KT, c.ident], writes=[PK_], inc=(i8 == 7))
                for tt in range(4):
                    p.op("dve", lambda e: e.tensor_scalar(out=Kd.t[:, tt, :], in0=pkv[:, tt * 256:(tt + 1) * 256],
                                                          scalar1=kdec.t[:, tt:tt + 1], scalar2=None, op0=ALU.mult),
                         reads=[PK_, kdec], writes=[Kd])
                psO = [psb[4 + ec] for ec in range(4)]
                started = [False] * 4
                if sc > 0:
                    for ec in range(4):
                        for dc in range(2):
                            p.op("pe", lambda e: e.matmul(psO[ec].t[:], lhsT=Rb.t[:, dc, ec * 128:(ec + 1) * 128], rhs=QD.t[:, dc, :],
                                                          start=(dc == 0), stop=False),
                                 reads=[Rb, QD], writes=[psO[ec]], inc=(dc == 1))
                        started[ec] = True
                for kt in range(4):
                    c0 = 128 * kt
                    PS_ = psb[2 + kt % 2]
                    for cq in range(2):
                        p.op("pe", lambda e: e.matmul(PS_.t[:, c0:512], lhsT=KT.t[:, cq, kt * 128:(kt + 1) * 128],
                                                      rhs=QT.t[:, cq, c0:512], start=(cq == 0), stop=(cq == 1)),
                             reads=[KT, QT], writes=[PS_], inc=(cq == 1))
                    T_ = PT[kt % 2]
                    p.op("dve", lambda e: e.tensor_tensor(out=T_.t[:, c0:512], in0=PS_.t[:, c0:512], in1=dtab.t[:, kt, c0:512],
                                                          op=ALU.mult), reads=[PS_, dtab], writes=[T_])
                    for ec in range(4):
                        p.op("pe", lambda e: e.matmul(psO[ec].t[:, c0:512], lhsT=V.t[:, kt, ec * 128:(ec + 1) * 128],
                                                      rhs=T_.t[:, c0:512], start=(not started[ec]), stop=(kt == 3)),
                             reads=[V, T_], writes=[psO[ec]], inc=(ec == 3 or kt == 3))
                        started[ec] = True
                for ec in range(4):
                    p.op("act", lambda e: e.activation(out=o_.t[:, ec, :], in_=psO[ec].t[:], func=AF.Copy),
                         reads=[psO[ec]], writes=[o_])
                if sc < NB - 1:
                    for dc in range(2):
                        P_ = psb[dc]
                        for tt in range(4):
                            p.op("pe", lambda e: e.matmul(P_.t[:], lhsT=Kd.t[:, tt, dc * 128:(dc + 1) * 128], rhs=V.t[:, tt, :],
                                                          start=(tt == 0), stop=(tt == 3)),
                                 reads=[Kd, V], writes=[P_], inc=(tt == 3))
                        if sc == 0:
                            p.op("dve", lambda e: e.tensor_copy(out=R32.t[:, dc, :], in_=P_.t[:]), reads=[P_], writes=[R32])
                        else:
                            p.op("dve", lambda e: e.scalar_tensor_tensor(out=R32.t[:, dc, :], in0=R32.t[:, dc, :], scalar=float(g512),
                                                                         in1=P_.t[:], op0=ALU.mult, op1=ALU.add),
                                 reads=[R32, P_], writes=[R32])
                    p.op("act", lambda e: e.activation(out=Rb.t[:], in_=R32.t[:], func=AF.Copy), reads=[R32], writes=[Rb])
                p.op("act", lambda e: e.activation(out=sq.t[:], in_=o_.t[:], func=AF.Square), reads=[o_], writes=[sq])
                PN_ = psb[2]
                for ec in range(4):
                    p.op("pe", lambda e: e.matmul(PN_.t[:], lhsT=ones.t[:], rhs=sq.t[:, ec, :], start=(ec == 0), stop=(ec == 3)),
                         reads=[ones, sq], writes=[PN_], inc=(ec == 3))
                p.op("dve", lambda e: e.tensor_scalar(out=rs.t[:], in0=PN_.t[:], scalar1=1.0 / 512.0, scalar2=EPS,
                                                      op0=ALU.mult, op1=ALU.add), reads=[PN_], writes=[rs])
                p.op("act", lambda e: e.activation(out=rs.t[:], in_=rs.t[:], func=AF.Sqrt), reads=[rs], writes=[rs])
                p.op("dve", lambda e: e.reciprocal(out=rs.t[:], in_=rs.t[:]), reads=[rs], writes=[rs])
                OG_ = og[nog % 2]
                nog += 1
                for ec in range(4):
                    p.op("dve", lambda e: e.tensor_tensor(out=o_.t[:, ec, :], in0=o_.t[:, ec, :], in1=rs.t[:], op=ALU.mult),
                         reads=[o_, rs], writes=[o_])
                p.op("dve", lambda e: e.tensor_tensor(out=OG_.t[:], in0=o_.t[:], in1=G.t[:], op=ALU.mult),
                     reads=[o_, G], writes=[OG_])
                p.dma("sp", c.OT[4 * h:4 * h + 4, :, cols].rearrange("c p t -> p c t"), OG_.t[:],
                      reads=[OG_], writes=[c.OTk[4 * sc + i] for i in range(4)])
        p.barrier()


def range_reduce(p, out, in_, ti, tf, shape_ap=lambda t: t.t[:], pre_add=None):
    o, i_, a, b = shape_ap(out), shape_ap(in_), shape_ap(ti), shape_ap(tf)
    src = in_
    if pre_add is not None:
        p.op("dve", lambda e: e.tensor_scalar(out=o, in0=i_, scalar1=float(pre_add), scalar2=None, op0=ALU.add),
             reads=[in_], writes=[out])
        i_ = o
        src = out
    p.op("dve", lambda e: e.tensor_scalar(out=a, in0=i_, scalar1=1.0 / TWO_PI, scalar2=None, op0=ALU.mult),
         reads=[src], writes=[ti])
    p.op("dve", lambda e: e.tensor_copy(out=b, in_=a), reads=[ti], writes=[tf])
    p.op("dve", lambda e: e.scalar_tensor_tensor(out=o, in0=b, scalar=-TWO_PI, in1=i_, op0=ALU.mult, op1=ALU.add),
         reads=[tf, src], writes=[out])
    p.op("dve", lambda e: e.tensor_scalar(out=b, in0=o, scalar1=math.pi, scalar2=-TWO_PI, op0=ALU.is_gt, op1=ALU.mult),
         reads=[out], writes=[tf])
    p.op("dve", lambda e: e.tensor_tensor(out=o, in0=o, in1=b, op=ALU.add), reads=[out, tf], writes=[out])
    p.op("dve", lambda e: e.tensor_scalar(out=b, in0=o, scalar1=-math.pi, scalar2=TWO_PI, op0=ALU.is_lt, op1=ALU.mult),
         reads=[out], writes=[tf])
    p.op("dve", lambda e: e.tensor_tensor(out=o, in0=o, in1=b, op=ALU.add), reads=[out, tf], writes=[out])


def phase_s5(c, L):
    p, nc = c.p, c.nc
    S, NT, NB = c.S, c.NT, c.NB
    NG = L["NG"]
    NK = NG // 8
    NS = NG // 2
    assert NG == 128
    psb = c.psb
    with contextlib.ExitStack() as ph:
        Bl = p.sb([128, NK, 2, 128], BF16, ph)
        BlO = p.sb([128, NK, 2, 128], BF16, ph)
        magst = p.sb([128, NS], F32, ph)
        thst = p.sb([128, NS], F32, ph)
        Cl = p.sb([128, NS, 2, 128], BF16, ph)
        J = p.sb([128, 512], F32, ph)
        dcol = p.sb([128, NK], F32, ph)
        pp = contextlib.ExitStack()
        idf = p.sb([128, 128], F32, pp)
        p.dma("sp", idf.t[:], c.ident_in, writes=[idf])
        are = p.sb([128, 64], F32, pp)
        aim = p.sb([128, 64], F32, pp)
        sc_ = p.sb([128, 4], F32, pp)
        p.dma("sp", are.t[:], L["a_re"], writes=[are])
        p.dma("sp", aim.t[:], L["a_im"], writes=[aim])
        p.dma("sp", sc_.t[:, 0:1], L["log_dt"].rearrange("(g o) -> g o", o=1), writes=[sc_])
        p.op("act", lambda e: e.activation(out=sc_.t[:, 1:2], in_=sc_.t[:, 0:1], func=AF.Exp), reads=[sc_], writes=[sc_])
        dt = sc_.t[:, 1:2]
        mag2 = p.sb([128, 2, 64], F32, pp)
        th2 = p.sb([128, 2, 64], F32, pp)
        tmp = p.sb([128, 64], F32, pp)
        ti = p.sb([128, 64], I32, pp)
        tf = p.sb([128, 64], F32, pp)
        thc = p.sb([128, 64], F32, pp)
        sn = p.sb([128, 64], F32, pp)
        cs = p.sb([128, 64], F32, pp)
        p.op("dve", lambda e: e.tensor_scalar(out=tmp.t[:], in0=are.t[:], scalar1=dt, scalar2=None, op0=ALU.mult),
             reads=[are, sc_], writes=[tmp])
        p.op("act", lambda e: e.activation(out=mag2.t[:, 0, :], in_=tmp.t[:], func=AF.Exp), reads=[tmp], writes=[mag2])
        p.op("act", lambda e: e.activation(out=mag2.t[:, 1, :], in_=tmp.t[:], func=AF.Exp), reads=[tmp], writes=[mag2])
        p.op("dve", lambda e: e.tensor_scalar(out=tmp.t[:], in0=aim.t[:], scalar1=dt, scalar2=None, op0=ALU.mult),
             reads=[aim, sc_, mag2], writes=[tmp])
        sl0 = lambda t: t.t[:, 0, :] if len(t.t.shape) == 3 else t.t[:]
        range_reduce(p, th2, tmp, ti, tf, shape_ap=sl0)
        p.op("dve", lambda e: e.tensor_copy(out=th2.t[:, 1, :], in_=th2.t[:, 0, :]), reads=[th2], writes=[th2])
        p.op("act", lambda e: e.activation(out=sn.t[:], in_=th2.t[:, 0, :], func=AF.Sin), reads=[th2], writes=[sn])
        range_reduce(p, thc, th2, ti, tf, shape_ap=sl0, pre_add=math.pi / 2)
        p.op("act", lambda e: e.activation(out=cs.t[:], in_=thc.t[:], func=AF.Sin), reads=[thc], writes=[cs])
        nr = p.sb([128, 64], F32, pp)
        ni = p.sb([128, 64], F32, pp)
        den = p.sb([128, 64], F32, pp)
        cre = p.sb([128, 64], F32, pp)
        cim = p.sb([128, 64], F32, pp)
        t2 = p.sb([128, 64], F32, pp)
        p.op("dve", lambda e: e.tensor_tensor(out=nr.t[:], in0=mag2.t[:, 0, :], in1=cs.t[:], op=ALU.mult), reads=[mag2, cs], writes=[nr])
        p.op("dve", lambda e: e.tensor_scalar(out=nr.t[:], in0=nr.t[:], scalar1=-1.0, scalar2=None, op0=ALU.add), reads=[nr], writes=[nr])
        p.op("dve", lambda e: e.tensor_tensor(out=ni.t[:], in0=mag2.t[:, 0, :], in1=sn.t[:], op=ALU.mult), reads=[mag2, sn], writes=[ni])
        p.op("dve", lambda e: e.tensor_tensor(out=den.t[:], in0=are.t[:], in1=are.t[:], op=ALU.mult), reads=[are], writes=[den])
        p.op("dve", lambda e: e.tensor_tensor(out=t2.t[:], in0=aim.t[:], in1=aim.t[:], op=ALU.mult), reads=[aim], writes=[t2])
        p.op("dve", lambda e: e.tensor_tensor(out=den.t[:], in0=den.t[:], in1=t2.t[:], op=ALU.add), reads=[den, t2], writes=[den])
        p.op("dve", lambda e: e.reciprocal(out=den.t[:], in_=den.t[:]), reads=[den], writes=[den])
        p.op("dve", lambda e: e.tensor_tensor(out=cre.t[:], in0=nr.t[:], in1=are.t[:], op=ALU.mult), reads=[nr, are], writes=[cre])
        p.op("dve", lambda e: e.tensor_tensor(out=t2.t[:], in0=ni.t[:], in1=aim.t[:], op=ALU.mult), reads=[ni, aim], writes=[t2])
        p.op("dve", lambda e: e.tensor_tensor(out=cre.t[:], in0=cre.t[:], in1=t2.t[:], op=ALU.add), reads=[cre, t2], writes=[cre])
        p.op("dve", lambda e: e.tensor_tensor(out=cre.t[:], in0=cre.t[:], in1=den.t[:], op=ALU.mult), reads=[cre, den], writes=[cre])
        p.op("dve", lambda e: e.tensor_tensor(out=cim.t[:], in0=ni.t[:], in1=are.t[:], op=ALU.mult), reads=[ni, are], writes=[cim])
        p.op("dve", lambda e: e.tensor_tensor(out=t2.t[:], in0=nr.t[:], in1=aim.t[:], op=ALU.mult), reads=[nr, aim], writes=[t2])
        p.op("dve", lambda e: e.tensor_tensor(out=cim.t[:], in0=cim.t[:], in1=t2.t[:], op=ALU.subtract), reads=[cim, t2], writes=[cim])
        p.op("dve", lambda e: e.tensor_tensor(out=cim.t[:], in0=cim.t[:], in1=den.t[:], op=ALU.mult), reads=[cim, den], writes=[cim])
        bre = p.sb([128, 64, 16], F32, pp)
        bim = p.sb([128, 64, 16], F32, pp)
        p.dma("sp", bre.t[:], L["b_re"], writes=[bre])
        p.dma("sp", bim.t[:], L["b_im"], writes=[bim])
        u1 = p.sb([128, 64, 16], F32, pp)
        u2 = p.sb([128, 64, 16], F32, pp)
        bb = p.sb([128, 2, 16, 64], BF16, pp)
        creB = cre.t[:].unsqueeze(2).broadcast_to([128, 64, 16])
        cimB = cim.t[:].unsqueeze(2).broadcast_to([128, 64, 16])
        p.op("dve", lambda e: e.tensor_tensor(out=u1.t[:], in0=bre.t[:], in1=creB, op=ALU.mult), reads=[bre, cre], writes=[u1])
        p.op("dve", lambda e: e.tensor_tensor(out=u2.t[:], in0=bim.t[:], in1=cimB, op=ALU.mult), reads=[bim, cim], writes=[u2])
        p.op("dve", lambda e: e.tensor_tensor(out=bb.t[:, 0, :, :].rearrange("g c q -> g q c"), in0=u1.t[:], in1=u2.t[:], op=ALU.subtract),
             reads=[u1, u2], writes=[bb])
        p.op("dve", lambda e: e.tensor_tensor(out=u1.t[:], in0=bim.t[:], in1=creB, op=ALU.mult), reads=[bim, cre, bb], writes=[u1])
        p.op("dve", lambda e: e.tensor_tensor(out=u2.t[:], in0=bre.t[:], in1=cimB, op=ALU.mult), reads=[bre, cim, bb], writes=[u2])
        p.op("dve", lambda e: e.tensor_tensor(out=bb.t[:, 1, :, :].rearrange("g c q -> g q c"), in0=u1.t[:], in1=u2.t[:], op=ALU.add),
             reads=[u1, u2], writes=[bb])
        bbk = Tok()
        p.dma("sp", c.BBS.rearrange("r g c q -> g r c q"), bb.t[:], reads=[bb], writes=[bbk])
        p.op("dve", lambda e: e.memset(Bl.t[:], 0.0), writes=[Bl])
        p.op("dve", lambda e: e.memset(BlO.t[:], 0.0), writes=[BlO])
        bbv = c.BBS.rearrange("r (k e) c q -> e c k r q", e=8)
        for g8 in range(8):
            for ri in range(2):
                p.dma("sp", Bl.t[16 * g8:16 * g8 + 16, :, ri, 64 * (g8 % 2):64 * (g8 % 2) + 64], bbv[g8][:, :, ri, :],
                      reads=[bbk], writes=[Bl])
                if (g8 // 2) % 2 == 1:
                    p.dma("sp", BlO.t[16 * g8:16 * g8 + 16, :, ri, 64 * (g8 % 2):64 * (g8 % 2) + 64], bbv[g8][:, :, ri, :],
                          reads=[bbk], writes=[BlO])
        for src, dst in ((mag2, magst), (th2, thst)):
            P_ = psb[0]
            p.op("pe", lambda e: e.transpose(out=P_.t[:, 0:128], in_=src.t[:].rearrange("g a q -> g (a q)"), identity=idf.t[:]),
                 reads=[src, idf], writes=[P_])
            pv = P_.t[:, 0:128].rearrange("p (s two) -> p s two", two=2)
            p.op("dve", lambda e: e.tensor_copy(out=dst.t[0:64, :], in_=pv[0:64, :, 0]), reads=[P_], writes=[dst])
            p.op("dve", lambda e: e.tensor_copy(out=dst.t[64:128, :], in_=pv[64:128, :, 1]), reads=[P_], writes=[dst])
        p.op("dve", lambda e: e.memset(Cl.t[:], 0.0), writes=[Cl])
        cd = p.sb([128, 16, 2, 64], F32, pp)
        for ri, src_ap in ((0, L["c_re"]), (1, L["c_im"])):
            for dup in range(2):
                p.dma("sp", cd.t[:, :, dup, :], src_ap, writes=[cd])
            sgn = 1.0 if ri == 0 else -1.0
            for q4 in range(4):
                P_ = psb[q4 % 2]
                for j in range(4):
                    cp = q4 * 4 + j
                    p.op("pe", lambda e: e.transpose(out=P_.t[:, j * 128:(j + 1) * 128],
                                                     in_=cd.t[:, cp, :, :].rearrange("g a q -> g (a q)"), identity=idf.t[:]),
                         reads=[cd, idf], writes=[P_], inc=(j == 3))
                pv = P_.t[:].rearrange("p (j s two) -> p s two j", j=4, two=2)
                clv = Cl.t[:].rearrange("p (k f) r n -> p k f r n", f=4)
                pv5 = P_.t[:].rearrange("p (j k f two) -> p k f two j", j=4, f=4, two=2)
                for s4 in range(4):
                    p.op("dve", lambda e: e.tensor_scalar(out=clv[0:64, :, s4, ri, 32 * s4 + q4 * 4:32 * s4 + q4 * 4 + 4],
                                                          in0=pv5[0:64, :, s4, 0, :], scalar1=sgn, scalar2=None, op0=ALU.mult),
                         reads=[P_], writes=[Cl])
                    p.op("dve", lambda e: e.tensor_scalar(out=clv[64:128, :, s4, ri, 32 * s4 + 16 + q4 * 4:32 * s4 + 16 + q4 * 4 + 4],
                                                          in0=pv5[64:128, :, s4, 1, :], scalar1=sgn, scalar2=None, op0=ALU.mult),
                         reads=[P_], writes=[Cl])
        p.dma("sp", dcol.t[:], L["d"].rearrange("(c p) -> p c", p=128), writes=[dcol], allow_slow_non_contiguous=True)
        p.dma("sp", J.t[:], L["iota"].broadcast_to([128, 512]), writes=[J])
        p.barrier()
        pp.close()
        hk = [p.sb([128, S], BF16, ph) for _ in range(2)]
        gk = [p.sb([128, S], BF16, ph) for _ in range(2)]
        cosT = [p.sb([128, 512], F32, ph) for _ in range(4)]
        sinT = [p.sb([128, 512], F32, ph) for _ in range(4)]
        rT = [p.sb([128, 512], F32, ph) for _ in range(4)]
        ang = p.sb([128, 512], F32, ph)
        angi = p.sb([128, 512], I32, ph)
        angf = p.sb([128, 512], F32, ph)
        ang2 = p.sb([128, 512], F32, ph)
        carry = [p.sb([128, 2], F32, ph) for _ in range(4)]
        d3 = [p.sb([128, 512], F32, ph) for _ in range(3)]
        d4 = [p.sb([128, 512], F32, ph) for _ in range(3)]
        m1 = [p.sb([128, 512], F32, ph) for _ in range(2)]
        m2 = [p.sb([128, 512], F32, ph) for _ in range(2)]
        d1 = [p.sb([128, 512], F32, ph) for _ in range(2)]
        d2 = [p.sb([128, 512], F32, ph) for _ in range(2)]
        mre = [p.sb([128, 512], F32, ph) for _ in range(2)]
        mim = [p.sb([128, 512], F32, ph) for _ in range(2)]
        wre = [p.sb([128, 512], F32, ph) for _ in range(2)]
        wim = [p.sb([128, 512], F32, ph) for _ in range(2)]
        xre = [p.sb([128, 512], F32, ph) for _ in range(3)]
        xim = [p.sb([128, 512], F32, ph) for _ in range(3)]
        xrb = [p.sb([128, 512], BF16, ph) for _ in range(3)]
        xib = [p.sb([128, 512], BF16, ph) for _ in range(3)]
        yv = p.sb([128, 512], F32, ph)
        y2 = p.sb([128, 512], F32, ph)
        sgm = p.sb([128, 512], F32, ph)
        n = 0
        for k in range(NK):
            HK_ = hk[k % 2]
            GK_ = gk[k % 2]
            p.dma("sp", HK_.t[:], c.HT[k, :, :], reads=c.HTk, writes=[HK_])
            for s4 in range(4):
                s = 4 * k + s4
                phi = thst.t[:, s:s + 1]
                p.op("dve", lambda e: e.tensor_scalar(out=ang.t[:], in0=J.t[:], scalar1=phi, scalar2=None, op0=ALU.mult),
                     reads=[J, thst], writes=[ang])
                range_reduce(p, ang2, ang, angi, angf)
                p.op("act", lambda e: e.activation(out=sinT[s4].t[:], in_=ang2.t[:], func=AF.Sin), reads=[ang2], writes=[sinT[s4]])
                range_reduce(p, ang2, ang, angi, angf, pre_add=math.pi / 2)
                p.op("act", lambda e: e.activation(out=cosT[s4].t[:], in_=ang2.t[:], func=AF.Sin), reads=[ang2], writes=[cosT[s4]])
                p.op("dve", lambda e: e.tensor_scalar(out=rT[s4].t[:], in0=J.t[:], scalar1=0.0, scalar2=magst.t[:, s:s + 1],
                                                      op0=ALU.mult, op1=ALU.add), reads=[J, magst], writes=[rT[s4]])
            pend = []

            def tail(s, s4, b3, PY_):
                p.op("dve", lambda e: e.tensor_tensor(out=xim[b3].t[:], in0=d3[b3].t[:], in1=d4[b3].t[:], op=ALU.add), reads=[d3[b3], d4[b3]], writes=[xim[b3]])
                p.op("act", lambda e: e.activation(out=carry[s4].t[:, 0:1], in_=xre[b3].t[:, 511:512], func=AF.Copy), reads=[xre[b3]], writes=[carry[s4]])
                p.op("act", lambda e: e.activation(out=carry[s4].t[:, 1:2], in_=xim[b3].t[:, 511:512], func=AF.Copy), reads=[xim[b3]], writes=[carry[s4]])
                p.op("act", lambda e: e.activation(out=xrb[b3].t[:], in_=xre[b3].t[:], func=AF.Copy), reads=[xre[b3]], writes=[xrb[b3]])
                p.op("act", lambda e: e.activation(out=xib[b3].t[:], in_=xim[b3].t[:], func=AF.Copy), reads=[xim[b3]], writes=[xib[b3]])
                p.op("pe", lambda e: e.matmul(PY_.t[:], lhsT=Cl.t[:, s, 0, :], rhs=xrb[b3].t[:], start=(s4 == 0), stop=False),
                     reads=[Cl, xrb[b3]], writes=[PY_], inc=False)
                p.op("pe", lambda e: e.matmul(PY_.t[:], lhsT=Cl.t[:, s, 1, :], rhs=xib[b3].t[:], start=False, stop=(s4 == 3)),
                     reads=[Cl, xib[b3]], writes=[PY_])

            def evac(cols, PY_):
                p.op("dve", lambda e: e.scalar_tensor_tensor(out=yv.t[:], in0=HK_.t[:, cols], scalar=dcol.t[:, k:k + 1], in1=PY_.t[:],
                                                             op0=ALU.mult, op1=ALU.add), reads=[HK_, dcol, PY_], writes=[yv])
                p.op("act", lambda e: e.activation(out=y2.t[:], in_=yv.t[:], func=AF.Square), reads=[yv], writes=[y2])
                p.op("act", lambda e: e.activation(out=y2.t[:], in_=y2.t[:], func=AF.Identity, scale=0.044715, bias=1.0), reads=[y2], writes=[y2])
                p.op("dve", lambda e: e.tensor_tensor(out=y2.t[:], in0=y2.t[:], in1=yv.t[:], op=ALU.mult), reads=[y2, yv], writes=[y2])
                p.op("act", lambda e: e.activation(out=sgm.t[:], in_=y2.t[:], func=AF.Sigmoid, scale=2.0 * math.sqrt(2.0 / math.pi)),
                     reads=[y2], writes=[sgm])
                p.op("dve", lambda e: e.tensor_tensor(out=GK_.t[:, cols], in0=yv.t[:], in1=sgm.t[:], op=ALU.mult),
                     reads=[yv, sgm], writes=[GK_])

            for blk in range(NB):
                cols = slice(blk * 512, (blk + 1) * 512)
                PY_ = psb[4 + blk % 2]
                for s4 in range(4):
                    s = 4 * k + s4
                    R = slice(32 * s4, 32 * s4 + 32)
                    b2 = n % 2
                    b3 = n % 3
                    n += 1
                    PR_, PI_ = psb[2 * b2], psb[2 * b2 + 1]
                    if s4 % 2 == 0:
                        BW_, RK = Bl, R
                    else:
                        BW_, RK = BlO, slice(64 * (s4 // 2), 64 * (s4 // 2) + 64)
                    p.op("pe", lambda e: e.matmul(PR_.t[:], lhsT=BW_.t[RK, k, 0, :], rhs=HK_.t[RK, cols], start=True, stop=True),
                         reads=[BW_, HK_], writes=[PR_])
                    p.op("pe", lambda e: e.matmul(PI_.t[:], lhsT=BW_.t[RK, k, 1, :], rhs=HK_.t[RK, cols], start=True, stop=True),
                         reads=[BW_, HK_], writes=[PI_])
                    CS_, SN_, RT_ = cosT[s4], sinT[s4], rT[s4]
                    p.op("dve", lambda e: e.tensor_tensor(out=m1[b2].t[:], in0=PR_.t[:], in1=CS_.t[:], op=ALU.mult), reads=[PR_, CS_], writes=[m1[b2]])
                    p.op("dve", lambda e: e.tensor_tensor(out=m2[b2].t[:], in0=PI_.t[:], in1=SN_.t[:], op=ALU.mult), reads=[PI_, SN_], writes=[m2[b2]])
                    p.op("dve", lambda e: e.tensor_tensor(out=mre[b2].t[:], in0=m1[b2].t[:], in1=m2[b2].t[:], op=ALU.add), reads=[m1[b2], m2[b2]], writes=[mre[b2]])
                    p.op("dve", lambda e: e.tensor_tensor(out=m1[b2].t[:], in0=PI_.t[:], in1=CS_.t[:], op=ALU.mult), reads=[PI_, CS_, mre[b2]], writes=[m1[b2]])
                    p.op("dve", lambda e: e.tensor_tensor(out=m2[b2].t[:], in0=PR_.t[:], in1=SN_.t[:], op=ALU.mult), reads=[PR_, SN_, mre[b2]], writes=[m2[b2]])
                    p.op("dve", lambda e: e.tensor_tensor(out=mim[b2].t[:], in0=m1[b2].t[:], in1=m2[b2].t[:], op=ALU.subtract), reads=[m1[b2], m2[b2]], writes=[mim[b2]])
                    while len(pend) > 1:
                        for fn in pend.pop(0):
                            fn()
                    ire = 0.0 if blk == 0 else carry[s4].t[:, 0:1]
                    iim = 0.0 if blk == 0 else carry[s4].t[:, 1:2]
                    p.op("dve", lambda e: e.tensor_tensor_scan(out=wre[b2].t[:], data0=RT_.t[:], data1=mre[b2].t[:], initial=ire,
                                                               op0=ALU.mult, op1=ALU.add), reads=[RT_, mre[b2], carry[s4]], writes=[wre[b2]])
                    p.op("dve", lambda e: e.tensor_tensor_scan(out=wim[b2].t[:], data0=RT_.t[:], data1=mim[b2].t[:], initial=iim,
                                                               op0=ALU.mult, op1=ALU.add), reads=[RT_, mim[b2], carry[s4]], writes=[wim[b2]])
                    DE = "pool"
                    p.op(DE, lambda e: e.tensor_tensor(out=d1[b2].t[:], in0=wre[b2].t[:], in1=CS_.t[:], op=ALU.mult), reads=[wre[b2], CS_], writes=[d1[b2]])
                    p.op(DE, lambda e: e.tensor_tensor(out=d2[b2].t[:], in0=wim[b2].t[:], in1=SN_.t[:], op=ALU.mult), reads=[wim[b2], SN_], writes=[d2[b2]])
                    p.op(DE, lambda e: e.tensor_tensor(out=xre[b3].t[:], in0=d1[b2].t[:], in1=d2[b2].t[:], op=ALU.subtract), reads=[d1[b2], d2[b2]], writes=[xre[b3]])
                    p.op(DE, lambda e: e.tensor_tensor(out=d3[b3].t[:], in0=wre[b2].t[:], in1=SN_.t[:], op=ALU.mult), reads=[wre[b2], SN_], writes=[d3[b3]])
                    p.op(DE, lambda e: e.tensor_tensor(out=d4[b3].t[:], in0=wim[b2].t[:], in1=CS_.t[:], op=ALU.mult), reads=[wim[b2], CS_], writes=[d4[b3]])
                    fns = [lambda s=s, s4=s4, b3=b3, PY_=PY_: tail(s, s4, b3, PY_)]
                    if s4 == 3:
                        fns.append(lambda cols=cols, PY_=PY_: evac(cols, PY_))
                    pend.append(fns)
            for fns in pend:
                for fn in fns:
                    fn()
            pend = []
            p.dma("sp", c.OT[k, :, :], GK_.t[:], reads=[GK_], writes=c.OTk)
        p.barrier()


def phase_glu(c, w_ap, NFC):
    p, nc = c.p, c.nc
    wv_ = w_ap.rearrange("(f p) d -> p f d", p=128)
    with contextlib.ExitStack() as ph:
        wvl = [p.sb([128, NFC, 512], BF16, ph) for _ in range(2)]
        wgt = [p.sb([128, NFC, 512], BF16, ph) for _ in range(2)]
        oT = [p.sb([128, NFC, 512], BF16, ph) for _ in range(3)]
        sg = [p.sb([128, 512], F32, ph) for _ in range(2)]
        mo = [p.sb([128, 512], F32, ph) for _ in range(4)]
        items = [(cg, blk) for cg in range(4) for blk in range(c.NB)]

        xpb = [p.sb([128, 512], F32, ph) for _ in range(12)]

        def load(i):
            cg, blk = items[i]
            p.dma("sp", oT[i % 3].t[:], c.OT[0:NFC, :, blk * 512:(blk + 1) * 512].rearrange("c p t -> p c t"),
                  reads=[c.OTk[4 * blk + j] for j in range(4)], writes=[oT[i % 3]])
            for tt in range(4):
                t = 4 * blk + tt
                XP_ = xpb[(i % 3) * 4 + tt]
                p.dma("sp", XP_.t[:], c.X[t * 128:(t + 1) * 128, cg * 512:(cg + 1) * 512], reads=[c.Xk[t][cg]], writes=[XP_])

        def loadw(cg):
            WV_, WG_ = wvl[cg % 2], wgt[cg % 2]
            for f in range(0, NFC, 8):
                p.dma("pool", WV_.t[:, f:f + 8, :], wv_[:, f:f + 8, cg * 512:(cg + 1) * 512], writes=[WV_])
                p.dma("pool", WG_.t[:, f:f + 8, :], wv_[:, f:f + 8, D + cg * 512:D + (cg + 1) * 512], writes=[WG_])
        loadw(0)
        load(0)
        load(1)
        n = 0
        for i, (cg, blk) in enumerate(items):
            if blk == 0 and cg + 1 < 4:
                loadw(cg + 1)
            if i + 2 < len(items):
                load(i + 2)
            WV_, WG_ = wvl[cg % 2], wgt[cg % 2]
            O_ = oT[i % 3]
            for tt in range(4):
                t = 4 * blk + tt
                M_ = mo[n % 4]
                SG_ = sg[n % 2]
                PV_, PG_ = c.psb[2 * (n % 4)], c.psb[2 * (n % 4) + 1]
                n += 1
                for P_, W_ in ((PV_, WV_), (PG_, WG_)):
                    for f in range(NFC):
                        p.op("pe", lambda e: e.matmul(P_.t[:], lhsT=O_.t[:, f, tt * 128:(tt + 1) * 128], rhs=W_.t[:, f, :],
                                                      start=(f == 0), stop=(f == NFC - 1)),
                             reads=[O_, W_], writes=[P_], inc=(f == NFC - 1))
                p.op("act", lambda e: e.activation(out=SG_.t[:], in_=PG_.t[:], func=AF.Sigmoid), reads=[PG_], writes=[SG_])
                XP_ = xpb[(i % 3) * 4 + tt]
                p.op("dve", lambda e: e.tensor_tensor(out=M_.t[:], in0=PV_.t[:], in1=SG_.t[:], op=ALU.mult), reads=[PV_, SG_], writes=[M_])
                p.op("dve", lambda e: e.tensor_tensor(out=M_.t[:], in0=M_.t[:], in1=XP_.t[:], op=ALU.add), reads=[M_, XP_], writes=[M_])
                p.dma("sp", c.X[t * 128:(t + 1) * 128, cg * 512:(cg + 1) * 512], M_.t[:], reads=[M_], writes=[c.Xk[t][cg]])
        p.barrier()


def make_layers_and_inputs(I, kinds, S, batch=0):
    layers = []
    im = {"x": np.ascontiguousarray(I["x"][batch, :S]), "ident": np.eye(128, dtype=np.float32)}
    gains = []
    cnt = {"a": 0, "b": 0, "c": 0}
    for i, kind in enumerate(kinds):
        L = {"kind": kind}
        gains.append(I["norm_mix"][i])
        gains.append(I["norm_mlp"][i])
        im["w1_%d" % i] = I["mlp_w1"][i]
        im["w2_%d" % i] = I["mlp_w2"][i]
        if kind != "none":
            j = cnt[kind]
            cnt[kind] += 1
        if kind == "a":
            L["NH"] = 8
            L["lambda_init"] = 0.8 - 0.6 * math.exp(-0.3 * i)
            w = I["a_w_in"][j]
            im["a_wq_%d" % i] = np.ascontiguousarray(w[:, 0:D])
            im["a_wk_%d" % i] = np.ascontiguousarray(w[:, D:2 * D])
            im["a_wv_%d" % i] = np.ascontiguousarray(w[:, 2 * D:3 * D])
            im["a_wo_%d" % i] = I["a_w_out"][j]
            im["a_lam_%d" % i] = I["a_lambda"][j]
            im["a_subln_%d" % i] = I["a_subln"][j]
        if kind == "b":
            L["NG"] = 128
            for nm in ("a_re", "a_im", "log_dt", "b_re", "b_im", "c_re", "c_im", "d"):
                im["b_" + nm] = I["b_" + nm][j]
            im["b_wglu"] = I["b_w_glu"][j]
            im["b_iota"] = np.arange(1, 513, dtype=np.float32).reshape(1, 512)
        if kind == "c":
            L["NH"] = 8
            L["head0"] = 0
            w = I["c_w_in"][j]
            im["c_wq_%d" % i] = np.ascontiguousarray(w[:, 0:D])
            im["c_wk_%d" % i] = np.ascontiguousarray(w[:, D:2 * D])
            im["c_wv_%d" % i] = np.ascontiguousarray(w[:, 2 * D:4 * D])
            im["c_wg_%d" % i] = np.ascontiguousarray(w[:, 4 * D:6 * D])
            im["c_wo_%d" % i] = I["c_w_out"][j]
            cosT, sinT, dtab, qdec, kdecT, g512 = ret_consts(S)
            im["c_cos"], im["c_sin"], im["c_dtab"], im["c_qdec"], im["c_kdecT"] = cosT, sinT, dtab, qdec, kdecT
            L["g512"] = g512
        layers.append(L)
    gains.append(I["norm_final"])
    im["gains"] = np.ascontiguousarray(np.stack(gains).astype(np.float32))
    return layers, im

_CACHE = {}
KINDS = ["a", "b", "c", "a"]
SEQ = 4096
N_ACTIVE = 4


def kernel(**inputs):
    I = {k: np.asarray(v) for k, v in inputs.items()}
    S = SEQ
    layers, im0 = make_layers_and_inputs(I, KINDS, S, batch=0)
    key = (S, tuple(KINDS))
    if key not in _CACHE:
        nc, c = build_program(S, layers, 8192)
        _CACHE[key] = nc
    nc = _CACHE[key]
    in_maps = []
    for core in range(N_ACTIVE):
        im = dict(im0)
        im["x"] = np.ascontiguousarray(I["x"][core, :S]).astype(np.float32, copy=False)
        in_maps.append(im)
    res = run_bass_kernel_spmd(nc, in_maps, core_ids=list(range(N_ACTIVE)))
    out = np.stack([np.asarray(res.results[b]["y"]) for b in range(N_ACTIVE)], axis=0)
    return out.astype(np.float32, copy=False)
```
